# Optimizing a Trainium2 kernel written in Bass

```python
import math
import jax, jax.numpy as jnp
from jax import lax
import numpy as np

D_MODEL = 2048
BATCH = 4
SEQ = 2048
DEPTH = 4

N_MIXERS = 2
POOL_WINDOWS = (2, 4, 8, 16)
N_POOL_GROUPS = len(POOL_WINDOWS)
POOL_GROUP = D_MODEL // N_POOL_GROUPS
HEAD_DIM = 128
N_HEADS = D_MODEL // HEAD_DIM
MOBA_BLOCK = 256
MOBA_TOPK = 3
Q_CHUNK = 16
REL_BUCKETS = 32
REL_MAX_DIST = 128
D_FF = 5632
CONV_WIDTH = 3
EPS = 1e-6
NEG = -1e30
N_POOL_LAYERS = (DEPTH + N_MIXERS - 1) // N_MIXERS
N_MOBA_LAYERS = DEPTH // N_MIXERS

kernel_name = 'hybrid_pool_moba_convffn_adaln'


def rmsnorm(x, g):
    xf = x.astype(jnp.float32)
    y = xf * lax.rsqrt(jnp.mean(xf * xf, axis=-1, keepdims=True) + EPS)
    return (y * g.astype(jnp.float32)).astype(x.dtype)


def modulate(h, shift, scale):
    return h * (1 + scale[:, None, :]) + shift[:, None, :]


def rel_bucket(dist):
    n = jnp.maximum(dist, 0)
    max_exact = REL_BUCKETS // 2
    nf = jnp.maximum(n, 1).astype(jnp.float32)
    large = max_exact + (jnp.log(nf / max_exact) / math.log(REL_MAX_DIST / max_exact)
                         * (REL_BUCKETS - max_exact)).astype(jnp.int32)
    large = jnp.minimum(large, REL_BUCKETS - 1)
    return jnp.where(n < max_exact, n, large)


def pool_mixer(h, w_grp, layer_scale):
    B, S, D = h.shape
    hf = h.astype(jnp.float32)
    cs = jnp.concatenate([jnp.zeros((B, 1, D), jnp.float32), jnp.cumsum(hf, axis=1)], axis=1)
    t = jnp.arange(S)
    outs = []
    for g, w in enumerate(POOL_WINDOWS):
        sl = slice(g * POOL_GROUP, (g + 1) * POOL_GROUP)
        lo = jnp.maximum(t + 1 - w, 0)
        cnt = (t + 1 - lo).astype(jnp.float32)
        mean = (cs[:, 1:, sl] - cs[:, lo, sl]) / cnt[None, :, None]
        d = (mean - hf[:, :, sl]).astype(h.dtype)
        outs.append(d @ w_grp[g])
    return jnp.concatenate(outs, axis=-1) * layer_scale


def moba_attention(h, w_qkv, w_o, rel_table):
    B, S, D = h.shape
    nb = -(-S // MOBA_BLOCK)
    sp = nb * MOBA_BLOCK
    topk = min(MOBA_TOPK, nb)
    qkv = jnp.pad(h @ w_qkv, ((0, 0), (0, sp - S), (0, 0)))
    q, k, v = jnp.split(qkv, 3, axis=-1)

    def heads(a):
        return a.reshape(B, sp, N_HEADS, HEAD_DIM).transpose(0, 2, 1, 3)

    q = heads(q) * (HEAD_DIM ** -0.5)
    k, v = heads(k), heads(v)
    kb = k.reshape(B, N_HEADS, nb, MOBA_BLOCK, HEAD_DIM)
    vb = v.reshape(B, N_HEADS, nb, MOBA_BLOCK, HEAD_DIM)
    kmean = jnp.mean(kb.astype(jnp.float32), axis=3)
    qblk = jnp.arange(sp) // MOBA_BLOCK
    gate = jnp.einsum('bhsd,bhnd->bhsn', q.astype(jnp.float32), kmean)
    past = jnp.arange(nb)[None, :] < qblk[:, None]
    gate = jnp.where(past, gate, NEG)
    _, idx = lax.top_k(gate, topk)

    nq = sp // Q_CHUNK
    q_c = q.reshape(B, N_HEADS, nq, Q_CHUNK, HEAD_DIM).transpose(2, 0, 1, 3, 4)
    idx_c = idx.reshape(B, N_HEADS, nq, Q_CHUNK, topk).transpose(2, 0, 1, 3, 4)
    head_ix = jnp.arange(N_HEADS)[None, :, None, None, None]
    s_ix = jnp.arange(MOBA_BLOCK)

    def chunk(args):
        ci, qc, ic = args
        t = ci * Q_CHUNK + jnp.arange(Q_CHUNK)
        j = (ci * Q_CHUNK) // MOBA_BLOCK
        flat = ic.reshape(B, N_HEADS, Q_CHUNK * topk)[..., None, None]
        ks = jnp.take_along_axis(kb, flat, axis=2).reshape(B, N_HEADS, Q_CHUNK, topk, MOBA_BLOCK, HEAD_DIM)
        vs = jnp.take_along_axis(vb, flat, axis=2).reshape(B, N_HEADS, Q_CHUNK, topk, MOBA_BLOCK, HEAD_DIM)
        l_sel = jnp.einsum('bhqd,bhqksd->bhqks', qc, ks).astype(jnp.float32)
        dist_sel = t[:, None, None] - ic[..., None] * MOBA_BLOCK - s_ix
        l_sel = l_sel + rel_table[rel_bucket(dist_sel), head_ix].astype(jnp.float32)
        valid = ic < j
        l_sel = jnp.where(valid[..., None], l_sel, NEG).reshape(B, N_HEADS, Q_CHUNK, topk * MOBA_BLOCK)
        ko = lax.dynamic_index_in_dim(kb, j, axis=2, keepdims=False)
        vo = lax.dynamic_index_in_dim(vb, j, axis=2, keepdims=False)
        l_own = jnp.einsum('bhqd,bhsd->bhqs', qc, ko).astype(jnp.float32)
        dist_own = t[:, None] - (j * MOBA_BLOCK + s_ix)[None, :]
        l_own = l_own + rel_table[rel_bucket(dist_own)].astype(jnp.float32).transpose(2, 0, 1)[None]
        l_own = jnp.where((dist_own >= 0)[None, None], l_own, NEG)
        p = jax.nn.softmax(jnp.concatenate([l_sel, l_own], axis=-1), axis=-1).astype(v.dtype)
        p_sel = p[..., :topk * MOBA_BLOCK].reshape(B, N_HEADS, Q_CHUNK, topk, MOBA_BLOCK)
        p_own = p[..., topk * MOBA_BLOCK:]
        return (jnp.einsum('bhqks,bhqksd->bhqd', p_sel, vs)
                + jnp.einsum('bhqs,bhsd->bhqd', p_own, vo))

    o = lax.map(chunk, (jnp.arange(nq), q_c, idx_c))
    o = o.transpose(1, 0, 3, 2, 4).reshape(B, sp, D)[:, :S]
    return o @ w_o


def conv_ffn(h, w_up, conv_w, conv_b, w_down):
    u = h @ w_up
    C = u.shape[-1]
    u = lax.conv_general_dilated(u, conv_w[:, None, :], window_strides=(1,),
                                 padding=[(CONV_WIDTH - 1, 0)],
                                 dimension_numbers=('NWC', 'WIO', 'NWC'),
                                 feature_group_count=C) + conv_b
    val, gate = jnp.split(u, 2, axis=-1)
    return (jax.nn.silu(gate) * val) @ w_down


def setup_inputs(seed: int = 0) -> dict:
    key = jax.random.key(seed)
    ks = jax.random.split(key, 16)
    D, F, G = D_MODEL, D_FF, POOL_GROUP
    nrm = jax.random.normal
    return {
        'x': nrm(ks[0], (BATCH, SEQ, D), jnp.float32),
        'c': nrm(ks[1], (BATCH, D), jnp.float32),
        'norm_g': 1.0 + 0.05 * nrm(ks[2], (DEPTH, 2, D), jnp.float32),
        'w_ada': 0.5 * D ** -0.5 * nrm(ks[3], (DEPTH, D, 6 * D), jnp.float32),
        'b_ada': 0.02 * nrm(ks[4], (DEPTH, 6 * D), jnp.float32),
        'pool_w': G ** -0.5 * nrm(ks[5], (N_POOL_LAYERS, N_POOL_GROUPS, G, G), jnp.float32),
        'pool_scale': 1.0 + 0.1 * nrm(ks[6], (N_POOL_LAYERS, D), jnp.float32),
        'w_qkv': D ** -0.5 * nrm(ks[7], (N_MOBA_LAYERS, D, 3 * D), jnp.float32),
        'w_o': D ** -0.5 * nrm(ks[8], (N_MOBA_LAYERS, D, D), jnp.float32),
        'rel_table': 0.5 * nrm(ks[9], (REL_BUCKETS, N_HEADS), jnp.float32),
        'w_up': D ** -0.5 * nrm(ks[10], (DEPTH, D, 2 * F), jnp.float32),
        'conv_w': CONV_WIDTH ** -0.5 * nrm(ks[11], (DEPTH, CONV_WIDTH, 2 * F), jnp.float32),
        'conv_b': 0.02 * nrm(ks[12], (DEPTH, 2 * F), jnp.float32),
        'w_down': F ** -0.5 * nrm(ks[13], (DEPTH, F, D), jnp.float32),
        'final_g': 1.0 + 0.05 * nrm(ks[14], (D,), jnp.float32),
    }


def reference(x, c, norm_g, w_ada, b_ada, pool_w, pool_scale, w_qkv, w_o, rel_table,
              w_up, conv_w, conv_b, w_down, final_g):
    cond = jax.nn.silu(c)
    for i in range(DEPTH):
        mod = cond @ w_ada[i] + b_ada[i]
        sh1, sc1, g1, sh2, sc2, g2 = jnp.split(mod, 6, axis=-1)
        h = modulate(rmsnorm(x, norm_g[i, 0]), sh1, sc1)
        li = i // N_MIXERS
        if i % N_MIXERS == 0:
            y = pool_mixer(h, pool_w[li], pool_scale[li])
        else:
            y = moba_attention(h, w_qkv[li], w_o[li], rel_table)
        x = x + g1[:, None, :] * y
        h = modulate(rmsnorm(x, norm_g[i, 1]), sh2, sc2)
        x = x + g2[:, None, :] * conv_ffn(h, w_up[i], conv_w[i], conv_b[i], w_down[i])
    return rmsnorm(x, final_g)
```

```python
from contextlib import ExitStack
import math
import numpy as np
import concourse.bass as bass
import concourse.mybir as mybir
from concourse.bass_utils import run_bass_kernel_spmd

F32 = mybir.dt.float32
F32R = mybir.dt.float32r
AF = mybir.ActivationFunctionType
ALU = mybir.AluOpType
AX = mybir.AxisListType

ENGS = ("pe", "act", "dve", "pool", "sp")

D = 2048
KC = 16
FF = 5632
FC = 44
NF = 4
NP = FC // NF
TP = 512
TC = 1024
HW = 32
NH = 16
EPS = 1e-6
WINS = (2, 4, 8, 16)
NEGB = -30000.0


class Prog:
    def __init__(self, nc, same_engine_sync=True):
        self.nc = nc
        self.ops = []
        self.same_engine_sync = same_engine_sync
        self.stack = ExitStack()
        self.arR = self.arF = None

    def sbuf(self, name, shape, dtype=F32):
        return self.stack.enter_context(self.nc.sbuf_tensor("sb_" + name, list(shape), dtype))

    def psum(self, name, shape, dtype=F32):
        return self.stack.enter_context(self.nc.psum_tensor("pp_" + name, list(shape), dtype))

    def setup_arenas(self, words_r, words_f):
        self.capR, self.capF = words_r, words_f
        self.arR = self.sbuf("arenaR", [128, words_r])
        self.arF = self.sbuf("arenaF", [128, words_f])
        self.offR = self.offF = 0

    def phase(self):
        self.offR = self.offF = 0
        self.ops.append(dict(barrier=True))

    def alloc(self, shape, r=False):
        words = 1
        for s_ in shape[1:]:
            words *= s_
        if r:
            off, ar, cap = self.offR, self.arR, self.capR
            self.offR += words
        else:
            off, ar, cap = self.offF, self.arF, self.capF
            self.offF += words
        assert off + words <= cap, ("arena overflow", r, off, words, cap)
        v = ar[0:shape[0], off:off + words]
        if len(shape) == 3:
            v = v.rearrange("p (a b) -> p a b", b=shape[2])
        elif len(shape) == 4:
            v = v.rearrange("p (a b c) -> p a b c", b=shape[2], c=shape[3])
        return v

    def op(self, eng, fn, reads=(), writes=()):
        self.ops.append(dict(eng=eng, fn=fn, reads=tuple(reads), writes=tuple(writes),
                             dma=False, semkey=None))

    def dma(self, q, out, in_, reads=(), writes=(), semkey=None, **kw):
        writes = tuple(writes)
        reads = tuple(reads)
        if semkey is None:
            semkey = ("dma", writes[0])
        self.ops.append(dict(eng=q, fn=lambda e: e.dma_start(out=out, in_=in_, **kw),
                             reads=reads, writes=writes, dma=True, semkey=semkey))

    def cc(self, kind, groups, in_ap, out_ap, reads, writes, semkey):
        self.ops.append(dict(eng="pool", fn=lambda e: e.collective_compute(kind, ALU.bypass, replica_groups=groups, ins=[in_ap], outs=[out_ap]),
                             reads=tuple(reads), writes=tuple(writes), dma=True, semkey=semkey, inc=1))

    def emit(self, final_wait_eng="sp"):
        nc = self.nc
        allops = self.ops
        ops = []
        bar_before = set()
        for o in allops:
            if o.get("barrier"):
                bar_before.add(len(ops))
            else:
                ops.append(o)
        n = len(ops)
        last_w = {}
        readers = {}
        deps = [None] * n
        last_on_eng = {}
        last_dma_key = {}
        pending_bar = {}
        for i, o in enumerate(ops):
            if i in bar_before:
                snap = set(last_on_eng.values()) | set(last_dma_key.values())
                for e in ENGS:
                    pending_bar[e] = set(snap)
            d = set()
            for r in o["reads"]:
                if r in last_w:
                    d.add(last_w[r])
            for w in o["writes"]:
                if w in last_w:
                    d.add(last_w[w])
                d.update(readers.get(w, ()))
            if pending_bar.get(o["eng"]):
                d.update(pending_bar[o["eng"]])
                pending_bar[o["eng"]] = None
            d.discard(i)
            deps[i] = d
            for r in o["reads"]:
                readers.setdefault(r, []).append(i)
            for w in o["writes"]:
                last_w[w] = i
                readers[w] = []
            if o["dma"]:
                last_dma_key[o["semkey"]] = i
            else:
                last_on_eng[o["eng"]] = i
        needed = [False] * n
        for i, o in enumerate(ops):
            best = {}
            pruned = set()
            for j in deps[i]:
                oj = ops[j]
                if oj["dma"]:
                    pruned.add(j)
                elif best.get(oj["eng"], -1) < j:
                    best[oj["eng"]] = j
            pruned.update(best.values())
            deps[i] = pruned
        for i, o in enumerate(ops):
            keep = set()
            for j in deps[i]:
                oj = ops[j]
                if oj["dma"]:
                    keep.add(j)
                    continue
                if oj["eng"] == o["eng"]:
                    if o["dma"]:
                        continue
                    if o["eng"] == "pe" or not self.same_engine_sync:
                        continue
                keep.add(j)
                needed[j] = True
            deps[i] = keep
        eng_sem = {e: self.stack.enter_context(nc.semaphore("s_" + e)) for e in ENGS}
        dma_sem = {}
        dma_cnt = {}
        tok = [None] * n
        eng_cnt = {e: 0 for e in ENGS}
        for i, o in enumerate(ops):
            if o["dma"]:
                k = o["semkey"]
                if k not in dma_sem:
                    dma_sem[k] = self.stack.enter_context(nc.semaphore("d%d" % len(dma_sem)))
                    dma_cnt[k] = 0
                dma_cnt[k] += o.get("inc", 16)
                tok[i] = (dma_sem[k], dma_cnt[k])
            elif needed[i]:
                eng_cnt[o["eng"]] += 1
                tok[i] = (eng_sem[o["eng"]], eng_cnt[o["eng"]])
        self.stats = dict(n_ops=n, n_dma_sems=len(dma_sem), eng_cnt=dict(eng_cnt), max_dma=max(dma_cnt.values()) if dma_cnt else 0)
        per_eng = {e: [] for e in ENGS}
        for i, o in enumerate(ops):
            per_eng[o["eng"]].append(i)
        block = self.stack.enter_context(nc.Block())

        def run(e_name, eng):
            waited = {}
            for i in per_eng[e_name]:
                o = ops[i]
                ws = {}
                for j in deps[i]:
                    s, v = tok[j]
                    key = id(s)
                    if key not in ws or ws[key][1] < v:
                        ws[key] = (s, v)
                for key, (s, v) in ws.items():
                    if waited.get(key, 0) >= v:
                        continue
                    eng.wait_ge(s, v)
                    waited[key] = v
                ins = o["fn"](eng)
                if tok[i] is not None:
                    ins.then_inc(tok[i][0], o.get("inc", 16) if o["dma"] else 1)
            if e_name == final_wait_eng:
                for k, s in dma_sem.items():
                    if waited.get(id(s), 0) < dma_cnt[k]:
                        eng.wait_ge(s, dma_cnt[k])

        @block.sync
        def _(eng):
            run("sp", eng)

        @block.scalar
        def _(eng):
            run("act", eng)

        @block.vector
        def _(eng):
            run("dve", eng)

        @block.gpsimd
        def _(eng):
            run("pool", eng)

        @block.tensor
        def _(eng):
            run("pe", eng)

    def close(self):
        self.stack.close()


def new_nc():
    nc = bass.Bass("TRN2", target_bir_lowering=False)
    nc.dge_precook = False
    return nc


def R(ap):
    return ap.bitcast(F32R)


class Ctx:
    def fork(self):
        n = Ctx()
        n.__dict__.update(self.__dict__)
        return n


def emit_norm(P, c, xsrc, xkeyf, col0, width, hdst0, A, Bv, tag, mask=None):
    W = width
    ssq = c.ps_n
    for kc in range(KC):
        P.op("act", lambda e, kc=kc: e.activation(out=R(c.sq[:, 0:W]), in_=xsrc[:, kc, col0:col0 + W], func=AF.Square),
             reads=[xkeyf(kc)], writes=["sil"])
        P.op("pe", lambda e, kc=kc: e.matmul(ssq[:, 0:W], R(c.ones[:, :]), R(c.sq[:, 0:W]), start=(kc == 0), stop=(kc == KC - 1)),
             reads=["sil", "ones"], writes=["ps_n"])
    P.op("dve", lambda e: e.tensor_scalar(out=c.rstd[:, 0:W], in0=ssq[:, 0:W], scalar1=1.0 / D, scalar2=EPS, op0=ALU.mult, op1=ALU.add),
         reads=["ps_n"], writes=["rstd"])
    P.op("act", lambda e: e.activation(out=c.rstd[:, 0:W], in_=c.rstd[:, 0:W], func=AF.Sqrt), reads=["rstd"], writes=["rstd"])
    P.op("dve", lambda e: e.reciprocal(out=c.rstd[:, 0:W], in_=c.rstd[:, 0:W]), reads=["rstd"], writes=["rstd"])
    if mask is not None:
        P.op("dve", lambda e: e.tensor_scalar(out=c.rstd[:, 0:W], in0=c.rstd[:, 0:W], scalar1=mask, scalar2=None, op0=ALU.mult),
             reads=["rstd", "scal"], writes=["rstd"])
    for kc in range(KC):
        tb = "tbuf%d" % (kc % 2)
        tt = c.ntmp[kc % 2]
        P.op("pool", lambda e, kc=kc, tt=tt: e.tensor_tensor(out=tt[:, 0:W], in0=xsrc[:, kc, col0:col0 + W], in1=c.rstd[:, 0:W], op=ALU.mult),
             reads=[xkeyf(kc), "rstd"], writes=[tb])
        if mask is None:
            P.op("act", lambda e, kc=kc, tt=tt: e.activation(out=R(c.hT[:, kc, hdst0:hdst0 + W]), in_=tt[:, 0:W], func=AF.Identity,
                                                             bias=Bv[:, kc:kc + 1], scale=A[:, kc:kc + 1]),
                 reads=[tb, tag], writes=[("hT", kc)])
        else:
            P.op("act", lambda e, kc=kc, tt=tt: e.activation(out=tt[:, 0:W], in_=tt[:, 0:W], func=AF.Identity,
                                                             bias=Bv[:, kc:kc + 1], scale=A[:, kc:kc + 1]),
                 reads=[tb, tag], writes=[tb])
            P.op("dve", lambda e, kc=kc, tt=tt: e.tensor_scalar(out=R(c.hT[:, kc, hdst0:hdst0 + W]), in0=tt[:, 0:W], scalar1=mask, scalar2=None, op0=ALU.mult),
                 reads=[tb, "scal"], writes=[("hT", kc)])


def emit_ffn_pass(P, c, p, ffn_w):
    wup_d, wdn_d = ffn_w
    cols = slice(p * TP, (p + 1) * TP)
    hkeys = [("hT", kc) for kc in range(KC)]

    def up_chunk(fc):
        wb = fc % 2
        wt = c.wup[wb]
        P.dma("sp", R(wt[:]), R(wup_d[fc]), writes=[("wup", wb)])
        for half, (pm, ph) in enumerate(((c.ps_v, c.ps_hv), (c.ps_g, c.ps_hg))):
            pk = "ps_v" if half == 0 else "ps_g"
            phk = "ps_hv" if half == 0 else "ps_hg"
            for kc in range(KC):
                P.op("pe", lambda e, kc=kc, pm=pm, half=half, wt=wt: e.matmul(pm[:, :], R(wt[:, kc, half * 128:(half + 1) * 128]),
                                                                              R(c.hT[:, kc, HW:HW + TP]), start=(kc == 0), stop=(kc == KC - 1)),
                     reads=[("wup", wb)] + hkeys, writes=[pk])
            ub = c.ubuf[half]
            uk = "ubuf%d" % half
            j = fc + half * FC
            if p == 0:
                for kc in range(KC):
                    P.op("pe", lambda e, kc=kc, ph=ph, half=half, wt=wt: e.matmul(ph[:, 0:2], R(wt[:, kc, half * 128:(half + 1) * 128]),
                                                                                  R(c.hT[:, kc, HW - 2:HW]), start=(kc == 0), stop=(kc == KC - 1)),
                         reads=[("wup", wb)] + hkeys, writes=[phk])
            P.op("act", lambda e, ub=ub, pm=pm: e.copy(out=ub[:, 2:2 + TP], in_=pm[:, :]), reads=[pk], writes=[uk])
            if p == 0:
                P.op("act", lambda e, ub=ub, ph=ph: e.copy(out=ub[:, 0:2], in_=ph[:, 0:2]), reads=[phk], writes=[uk + "h"])
                P.op("pool", lambda e, ub=ub, j=j: e.tensor_copy(out=c.usave[:, j, :], in_=ub[:, TP:TP + 2]), reads=[uk], writes=[("usave", j)])
            else:
                P.op("pool", lambda e, ub=ub, j=j: e.tensor_copy(out=ub[:, 0:2], in_=c.usave[:, j, :]), reads=[("usave", j)], writes=[uk + "h"])
            tt = c.tbuf[half]
            tk = "tbuf%d" % half
            P.op("act", lambda e, pm=pm, tt=tt, j=j: e.activation(out=tt[:, :], in_=pm[:, :], func=AF.Identity,
                                                                bias=c.convb[:, j:j + 1], scale=c.convw[:, 2, j:j + 1]),
                 reads=[pk, "convw"], writes=[tk])
            P.op("dve", lambda e, ub=ub, tt=tt, j=j: e.scalar_tensor_tensor(out=tt[:, :], in0=ub[:, 1:1 + TP], scalar=c.convw[:, 1, j:j + 1],
                                                                            in1=tt[:, :], op0=ALU.mult, op1=ALU.add),
                 reads=[uk, uk + "h", tk, "convw"], writes=[tk])
            P.op("dve", lambda e, ub=ub, tt=tt, j=j: e.scalar_tensor_tensor(out=tt[:, :], in0=ub[:, 0:TP], scalar=c.convw[:, 0, j:j + 1],
                                                                            in1=tt[:, :], op0=ALU.mult, op1=ALU.add),
                 reads=[uk, uk + "h", tk, "convw"], writes=[tk])
        P.op("act", lambda e: e.activation(out=R(c.sil[:, :]), in_=c.tbuf[1][:, :], func=AF.Silu), reads=["tbuf1"], writes=["sil"])
        ab = fc % (2 * NF)
        P.op("pool", lambda e, ab=ab: e.tensor_tensor(out=R(c.aT[:, ab, :]), in0=c.sil[:, :], in1=c.tbuf[0][:, :], op=ALU.mult),
             reads=["sil", "tbuf0"], writes=[("aT", ab)])

    def down_piece(pc):
        for dh in range(2):
            wb = dh
            wt = c.wdn[wb]
            P.dma("sp", R(wt[:]), R(wdn_d[pc, dh]), writes=[("wdn", wb)])
            for dl in range(8):
                dch = dh * 8 + dl
                pb = dch % 2
                pd = c.ps_d[pb]
                for f in range(NF):
                    ab = (pc * NF + f) % (2 * NF)
                    P.op("pe", lambda e, f=f, ab=ab, pd=pd, dl=dl, wt=wt: e.matmul(pd[:, :], R(wt[:, f, dl * 128:(dl + 1) * 128]), R(c.aT[:, ab, :]),
                                                                                    start=(f == 0), stop=(f == NF - 1)),
                         reads=[("wdn", wb), ("aT", ab)], writes=[("ps_d", pb)])
                if dch % 2 == 0:
                    P.op("dve", lambda e, pd=pd, dch=dch: e.scalar_tensor_tensor(out=c.xT[:, dch, cols], in0=pd[:, :], scalar=c.modT[:, 80 + dch:81 + dch],
                                                                                  in1=c.xT[:, dch, cols], op0=ALU.mult, op1=ALU.add),
                         reads=[("ps_d", pb), "modT", ("xT", p, dch)], writes=[("xT", p, dch)])
                else:
                    P.op("act", lambda e, pd=pd, dch=dch: e.activation(out=c.dtmp[:, :], in_=pd[:, :], func=AF.Copy, scale=c.modT[:, 80 + dch:81 + dch]),
                         reads=[("ps_d", pb), "modT"], writes=["rstd"])
                    P.op("pool", lambda e, dch=dch: e.tensor_tensor(out=c.xT[:, dch, cols], in0=c.dtmp[:, :], in1=c.xT[:, dch, cols], op=ALU.add),
                         reads=["rstd", ("xT", p, dch)], writes=[("xT", p, dch)])

    for pc in range(NP):
        for f in range(NF):
            up_chunk(pc * NF + f)
        if pc >= 1:
            down_piece(pc - 1)
    down_piece(NP - 1)


def emit_final_norm(P, c, p, ydst):
    emit_norm(P, c, c.xT, (lambda kc, p=p: ("xT", p, kc)), p * TP, TP, HW, c.vecs[:, 48:64], c.zeros16, "vecs")
    P.dma("sp", ydst[:, :, p * TP:(p + 1) * TP], c.hT[:, :, HW:HW + TP], reads=[("hT", kc) for kc in range(KC)], writes=[("yout", p)])


def emit_ones(P, c):
    P.op("pool", lambda e: e.memset(c.ones0[:, :], 1.0), writes=["ones0"])
    P.op("dve", lambda e: e.tensor_copy(out=R(c.ones[:, :]), in_=c.ones0[:, :]), reads=["ones0"], writes=["ones"])


def halo_exchange(P, c, io, groups):
    xk = [("xT", 1, k) for k in range(KC)]
    P.dma("sp", io["xhi"].rearrange("p (k t) -> p k t", t=HW), c.xT[:, :, TC - HW:TC], reads=xk, writes=["xhi"])
    P.cc("AllGather", groups, io["xhi"], io["xho"], ["xhi"], ["xho"], "cc_xh")
    P.dma("sp", c.xh[:, :, :], io["xho"][0:128, :].rearrange("p (k t) -> p k t", t=HW), reads=["xho"], writes=["xh"])


def layer_body(P, c, io, i, mixer, final, groups):
    li = i // 2
    c = c.fork()
    P.phase()
    c.hT = P.alloc([128, KC, HW + TP], r=True)
    c.wup = [P.alloc([128, KC, 256], r=True) for _ in range(2)]
    c.wdn = [P.alloc([128, NF, 1024], r=True) for _ in range(2)]
    c.aT = P.alloc([128, 2 * NF, TP], r=True)
    c.dT = P.alloc([128, 4, HW + TP], r=True)
    c.sil = P.alloc([128, TP], r=True)
    c.sq = c.sil
    c.ones = P.alloc([128, 128], r=True)
    c.ubuf = [P.alloc([128, 2 + TP]) for _ in range(2)]
    c.tbuf = [P.alloc([128, TP]) for _ in range(2)]
    c.ntmp = c.tbuf
    c.rstd = P.alloc([128, TP])
    c.dtmp = c.rstd
    c.fix = P.alloc([128, 4, 16])
    c.usave = P.alloc([128, 2 * FC, 2])
    c.convw = P.alloc([128, 3, 2 * FC])
    c.convb = P.alloc([128, 2 * FC])
    c.ic = P.alloc([128, 4, 4, 16])
    c.ones0 = P.alloc([128, 128])
    c.AB = P.alloc([128, 4, 16])
    c.zeros16 = P.alloc([128, 16])
    c.vecs = P.alloc([128, 64])
    c.pw = [c.wup[k][:, 0:8, :] for k in range(2)]
    c.sA = c.wdn[0][:, :, 0:HW + TP]
    c.sB = c.wdn[1][:, :, 0:HW + TP]
    c.ps_n, c.ps_v, c.ps_g, c.ps_hv, c.ps_hg = c.bank[0], c.bank[1], c.bank[2], c.bank[3], c.bank[4]
    c.ps_d = [c.bank[5], c.bank[6]]
    c.ps_p = c.ps_d
    c.ps_ph = c.ps_hv
    c.modT = c.modT_all[:, i, :]
    hmask = c.scal[:, 0:1]

    P.dma("sp", c.vecs[:, :], io["vecs"][i], writes=["vecs"])
    P.dma("sp", c.convw[:, :, :], io["convw"][i], writes=["convw"])
    P.dma("sp", c.convb[:, :], io["convb"][i], writes=["convw"], semkey=("dma", "convb"))
    emit_ones(P, c)
    P.op("pool", lambda e: e.memset(c.zeros16[:, :], 0.0), writes=["zeros16"])
    P.op("dve", lambda e: e.scalar_tensor_tensor(out=c.AB[:, 0, :], in0=c.modT[:, 16:32], scalar=1.0, in1=c.vecs[:, 0:16], op0=ALU.add, op1=ALU.mult),
         reads=["modT", "vecs"], writes=["AB"])
    P.op("dve", lambda e: e.scalar_tensor_tensor(out=c.AB[:, 1, :], in0=c.modT[:, 64:80], scalar=1.0, in1=c.vecs[:, 16:32], op0=ALU.add, op1=ALU.mult),
         reads=["modT", "vecs"], writes=["AB"])
    P.op("dve", lambda e: e.tensor_tensor(out=c.AB[:, 2, :], in0=c.modT[:, 32:48], in1=c.vecs[:, 32:48], op=ALU.mult),
         reads=["modT", "vecs"], writes=["AB"])
    halo_exchange(P, c, io, groups)

    if mixer == "pool":
        pw_d = io["poolw"][li]
        P.dma("sp", c.ic[:, :, :, :], io["ic"], writes=["ic"])
        for p in range(2):
            if p == 0:
                emit_norm(P, c, c.xh, (lambda kc: "xh"), 0, HW, 0, c.AB[:, 0, :], c.modT[:, 0:16], "AB", mask=hmask)
            else:
                for kc in range(KC):
                    P.op("pool", lambda e, kc=kc: e.tensor_copy(out=R(c.hT[:, kc, 0:HW]), in_=R(c.hT[:, kc, TP:TP + HW])),
                         reads=[("hT", kc)], writes=[("hT", kc)])
            emit_norm(P, c, c.xT, (lambda kc, p=p: ("xT", p, kc)), p * TP, TP, HW, c.AB[:, 0, :], c.modT[:, 0:16], "AB")
            WT = HW + TP
            for g in range(4):
                hg = c.hT[:, 4 * g:4 * g + 4, :]
                hk = [("hT", 4 * g + k) for k in range(4)]
                bufs = [c.sA, c.sB]
                src, srck = hg, hk
                sh = 1
                for st in range(g + 1):
                    dst = bufs[st % 2]
                    dk = ("wdn", st % 2)
                    P.op("pool", lambda e, dst=dst, src=src, sh=sh: e.tensor_tensor(out=R(dst[:, :, sh:WT]), in0=src[:, :, sh:WT], in1=src[:, :, 0:WT - sh], op=ALU.add),
                         reads=srck, writes=[dk])
                    P.op("pool", lambda e, dst=dst, src=src, sh=sh: e.tensor_copy(out=R(dst[:, :, 0:sh]), in_=src[:, :, 0:sh]),
                         reads=srck, writes=[dk])
                    src, srck = dst, [dk]
                    sh *= 2
                P.op("dve", lambda e, src=src, g=g, hg=hg: e.scalar_tensor_tensor(out=R(c.dT[:, :, :]), in0=src[:, :, :], scalar=1.0 / WINS[g],
                                                                                 in1=hg, op0=ALU.mult, op1=ALU.subtract),
                     reads=srck + hk, writes=["dT"])
                if p == 0:
                    P.op("dve", lambda e, src=src, g=g: e.tensor_tensor(out=c.fix[:, :, :], in0=src[:, :, HW:HW + 16], in1=c.ic[:, g, :, :], op=ALU.mult),
                         reads=srck + ["ic"], writes=["fix"])
                    P.op("dve", lambda e, hg=hg: e.tensor_tensor(out=R(c.dT[:, :, HW:HW + 16]), in0=c.fix[:, :, :], in1=hg[:, :, HW:HW + 16], op=ALU.subtract),
                         reads=["fix", "dT"] + hk, writes=["dT"])
                wb = g % 2
                P.dma("sp", R(c.pw[wb]), R(pw_d[g]), writes=[("wup", wb)])
                for m in range(4):
                    ch = 4 * g + m
                    pb = ch % 2
                    pd = c.ps_p[pb]
                    for kc in range(4):
                        P.op("pe", lambda e, kc=kc, m=m, pd=pd, wb=wb: e.matmul(pd[:, :], R(c.pw[wb][:, kc * 2 + m // 2, (m % 2) * 128:(m % 2) * 128 + 128]), R(c.dT[:, kc, HW:WT]),
                                                                                start=(kc == 0), stop=(kc == 3)),
                             reads=[("wup", wb), "dT"], writes=[("ps_d", pb)])
                    P.op("dve", lambda e, pd=pd, ch=ch, p=p: e.scalar_tensor_tensor(out=c.xT[:, ch, p * TP:(p + 1) * TP], in0=pd[:, :], scalar=c.AB[:, 2, ch:ch + 1],
                                                                                    in1=c.xT[:, ch, p * TP:(p + 1) * TP], op0=ALU.mult, op1=ALU.add),
                         reads=[("ps_d", pb), "AB", ("xT", p, ch)], writes=[("xT", p, ch)])
                    if p == 0:
                        for kc in range(4):
                            P.op("pe", lambda e, kc=kc, m=m, wb=wb: e.matmul(c.ps_ph[:, 0:HW], R(c.pw[wb][:, kc * 2 + m // 2, (m % 2) * 128:(m % 2) * 128 + 128]), R(c.dT[:, kc, 0:HW]),
                                                                            start=(kc == 0), stop=(kc == 3)),
                                 reads=[("wup", wb), "dT"], writes=["ps_hv"])
                        P.op("dve", lambda e, ch=ch: e.scalar_tensor_tensor(out=c.xh[:, ch, :], in0=c.ps_ph[:, 0:HW], scalar=c.AB[:, 2, ch:ch + 1],
                                                                            in1=c.xh[:, ch, :], op0=ALU.mult, op1=ALU.add),
                             reads=["ps_hv", "AB", "xh"], writes=["xh"])

    for p in range(2):
        if p == 0:
            emit_norm(P, c, c.xh, (lambda kc: "xh"), 0, HW, 0, c.AB[:, 1, :], c.modT[:, 48:64], "AB", mask=hmask)
        emit_norm(P, c, c.xT, (lambda kc, p=p: ("xT", p, kc)), p * TP, TP, HW, c.AB[:, 1, :], c.modT[:, 48:64], "AB")
        emit_ffn_pass(P, c, p, (io["wup"][i], io["wdn"][i]))
    if final:
        for p in range(2):
            emit_final_norm(P, c, p, io["yo"])


def qkv_body(P, c, io, i):
    li = i // 2
    c = c.fork()
    P.phase()
    hT2 = [P.alloc([128, KC, HW + TP], r=True) for _ in range(2)]
    c.w = [P.alloc([128, KC, 256], r=True) for _ in range(3)]
    c.sil = P.alloc([128, TP], r=True)
    c.sq = c.sil
    c.ones = P.alloc([128, 128], r=True)
    c.tbuf = [P.alloc([128, TP]) for _ in range(2)]
    c.ntmp = c.tbuf
    c.rstd = P.alloc([128, TP])
    c.st = [P.alloc([128, 512]) for _ in range(2)]
    c.ones0 = P.alloc([128, 128])
    c.AB = P.alloc([128, 4, 16])
    c.vecs = P.alloc([128, 64])
    c.ps_n = c.bank[0]
    c.ps = [c.bank[1], c.bank[2], c.bank[3]]
    c.modT = c.modT_all[:, i, :]
    w_d = io["wqkv"][li]
    P.dma("sp", c.vecs[:, :], io["vecs"][i], writes=["vecs"])
    emit_ones(P, c)
    P.op("dve", lambda e: e.scalar_tensor_tensor(out=c.AB[:, 0, :], in0=c.modT[:, 16:32], scalar=1.0, in1=c.vecs[:, 0:16], op0=ALU.add, op1=ALU.mult),
         reads=["modT", "vecs"], writes=["AB"])
    for p in range(2):
        cp = c.fork()
        cp.hT = hT2[p]
        emit_norm(P, cp, c.xT, (lambda kc, p=p: ("xT", p, kc)), p * TP, TP, HW, c.AB[:, 0, :], c.modT[:, 0:16], "AB")
    hkeys = [("hT", kc) for kc in range(KC)]
    cnt = 0
    outk = []
    scale_q = float(128 ** -0.5)
    for hg in range(24):
        wb = hg % 3
        wt = c.w[wb]
        P.dma("sp", R(wt[:, :, :]), R(w_d[hg]), writes=[("w", wb)])
        for p in range(2):
            hT = hT2[p]
            for m in range(2 if hg < 16 else 4):
                pb = cnt % 3
                sb = cnt % 2
                cnt += 1
                ps = c.ps[pb]
                st = c.st[sb]
                for kc in range(KC):
                    if hg < 16:
                        P.op("pe", lambda e, kc=kc, m=m, ps=ps, wt=wt, hT=hT: e.matmul(ps[:, :], R(wt[:, kc, m * 128:(m + 1) * 128]), R(hT[:, kc, HW:HW + TP]),
                                                                                       start=(kc == 0), stop=(kc == KC - 1)),
                             reads=[("w", wb)] + hkeys, writes=[("ps", pb)])
                    else:
                        P.op("pe", lambda e, kc=kc, m=m, ps=ps, wt=wt, hT=hT: e.matmul(ps[:, 0:256], R(hT[:, kc, HW + m * 128:HW + (m + 1) * 128]), R(wt[:, kc, :]),
                                                                                       start=(kc == 0), stop=(kc == KC - 1)),
                             reads=[("w", wb)] + hkeys, writes=[("ps", pb)])
                if hg < 8:
                    hd = hg * 2 + m
                    P.op("act", lambda e, ps=ps, st=st: e.activation(out=st[:, :], in_=ps[:, :], func=AF.Copy, scale=scale_q), reads=[("ps", pb)], writes=[("st", sb)])
                    dst = io["qo"][hd * 128:(hd + 1) * 128, p * TP:(p + 1) * TP]
                    key = ("qo", hd, p)
                    src = st[:, :]
                elif hg < 16:
                    hd = (hg - 8) * 2 + m
                    P.op("act", lambda e, ps=ps, st=st: e.copy(out=st[:, :], in_=ps[:, :]), reads=[("ps", pb)], writes=[("st", sb)])
                    dst = io["ko"][hd // 4][(hd % 4) * 128:(hd % 4 + 1) * 128, p * TP:(p + 1) * TP]
                    key = ("ko", hd, p)
                    src = st[:, :]
                else:
                    tt = p * 4 + m
                    P.op("act", lambda e, ps=ps, st=st: e.copy(out=st[:, 0:256], in_=ps[:, 0:256]), reads=[("ps", pb)], writes=[("st", sb)])
                    dst = io["vo"][tt // 2][(tt % 2) * 128:(tt % 2 + 1) * 128, (hg - 16) * 256:(hg - 15) * 256]
                    key = ("vo", tt, hg)
                    src = st[:, 0:256]
                outk.append(key)
                P.dma("sp", dst, src, reads=[("st", sb)], writes=[key], semkey=("o", sb))
    return outk


def attn_body(P, c, io, i, outk, groups):
    li = i // 2
    kkeys = [k for k in outk if k[0] == "ko"]
    vkeys = [k for k in outk if k[0] == "vo"]
    qkeys = [k for k in outk if k[0] == "qo"]
    for g in range(4):
        P.cc("AllGather", groups, io["ko"][g], io["kex"][g], [k for k in kkeys if k[1] // 4 == g], [("kex", g)], ("cc_k", g))
        P.cc("AllGather", groups, io["vo"][g], io["vex"][g], [k for k in vkeys if k[1] // 2 == g], [("vex", g)], ("cc_v", g))
    c = c.fork()
    P.phase()
    c.attnT = P.alloc([128, NH, TC], r=True)
    c.q = [P.alloc([128, TC], r=True) for _ in range(2)]
    c.k = [P.alloc([128, 8, 2, 128], r=True) for _ in range(2)]
    c.v = [P.alloc([128, 16, 128], r=True) for _ in range(2)]
    c.PT = [P.alloc([128, 256], r=True) for _ in range(3)]
    c.ET = P.alloc([128, TC], r=True)
    c.oh = P.alloc([128, 8, 128], r=True)
    c.kmR = P.alloc([128, 8], r=True)
    c.ones = P.alloc([128, 128], r=True)
    c.b = [P.alloc([128, 4, 256]) for _ in range(2)]
    c.km = P.alloc([128, 8])
    c.gm = P.alloc([128, 8, 8])
    c.g2 = P.alloc([128, 8, 8])
    c.eq = P.alloc([128, 8, 8])
    c.E = P.alloc([128, 8, 8])
    c.mx = P.alloc([128, 8])
    c.sS = [P.alloc([128, 256]) for _ in range(2)]
    c.rl = P.alloc([128, 256])
    c.cm = P.alloc([128, 3, 8, 8])
    c.t31 = P.alloc([128, NH])
    c.ident = P.alloc([128, 128])
    c.ones0 = P.alloc([128, 128])
    c.ps_gate = c.bank[0]
    c.ps_T = [c.bank[1], c.bank[2]]
    c.ps_s = [c.bank[3], c.bank[4]]
    c.ps_o = c.bank[5]
    c.ps_l = c.bank[6]
    c.modT = c.modT_all[:, i, :]
    q_d, b_d, wo_d = io["qo"], io["biasT"], io["wo"][li]
    P.dma("sp", c.cm[:, :, :, :], io["cmask"], writes=["cm"])
    P.dma("sp", c.t31[:, :], io["tab31"], writes=["t31"])
    P.dma("sp", R(c.oh[:, :, :]), R(io["oh"]), writes=["oh"])
    P.dma("sp", R(c.ET[:, :]), R(io["zeros"]), writes=["ET"])
    P.dma("sp", c.ident[:, :], io["ident"], writes=["ident"])
    emit_ones(P, c)

    def bc(ap2):
        return ap2.unsqueeze(2).to_broadcast([128, 8, 8])

    sidx = 0
    pidx = 0
    for h in range(NH):
        hb = h % 2
        q, k, v, b = c.q[hb], c.k[hb], c.v[hb], c.b[hb]
        hr = slice(h * 128, (h + 1) * 128)
        hc = slice(h * 128, (h + 1) * 128)
        P.dma("sp", R(q[:, :]), R(q_d[hr, :]), reads=qkeys, writes=[("q", hb)])
        hg = h // 4
        hq = slice((h % 4) * 128, (h % 4 + 1) * 128)
        P.dma("sp", R(k[:, 0:4, :, :]), R(io["kex"][hg][hq, :].rearrange("p (a b c) -> p a b c", b=2, c=128)), reads=[("kex", hg)], writes=[("kp", hb)])
        P.dma("sp", R(k[:, 4:8, :, :]), R(io["ko"][hg][hq, :].rearrange("p (a b c) -> p a b c", b=2, c=128)), reads=kkeys, writes=[("kn", hb)])
        for g in range(4):
            P.dma("sp", R(v[:, 2 * g:2 * g + 2, :]), R(io["vex"][g][0:256, hc].rearrange("(a t) d -> t a d", t=128)), reads=[("vex", g)],
                  writes=[("vp", hb, g)], semkey=("dma", "vp", hb, g))
            P.dma("sp", R(v[:, 8 + 2 * g:10 + 2 * g, :]), R(io["vo"][g][:, hc].rearrange("(a t) d -> t a d", t=128)), reads=vkeys,
                  writes=[("vn", hb, g)], semkey=("dma", "vn", hb, g))
        P.dma("sp", b[:, :, :], b_d[h], writes=[("b", hb)])
        kk = [("kp", hb), ("kn", hb)]
        vk = [("vp", hb, g) for g in range(4)] + [("vn", hb, g) for g in range(4)]
        P.op("dve", lambda e, k=k: e.tensor_reduce(out=c.km[:, :], in_=k[:, :, :, :], axis=AX.XY, op=ALU.add), reads=kk, writes=["km"])
        P.op("dve", lambda e: e.tensor_scalar(out=R(c.kmR[:, :]), in0=c.km[:, :], scalar1=1.0 / 256.0, scalar2=None, op0=ALU.mult), reads=["km"], writes=["kmR"])
        for qt in range(8):
            P.op("pe", lambda e, qt=qt, q=q: e.matmul(c.ps_gate[:, qt * 8:(qt + 1) * 8], R(q[:, qt * 128:(qt + 1) * 128]), R(c.kmR[:, :]), start=True, stop=True),
                 reads=[("q", hb), "kmR"], writes=["ps_gate"])
        gate3 = c.ps_gate[:, 0:64].rearrange("p (a b) -> p a b", b=8)
        P.op("dve", lambda e, gate3=gate3: e.tensor_tensor(out=c.gm[:, :, :], in0=gate3, in1=c.cm[:, 0, :, :], op=ALU.add), reads=["ps_gate", "cm"], writes=["gm"])
        src = c.gm
        srck = "gm"
        for it in range(2):
            P.op("dve", lambda e, src=src: e.tensor_reduce(out=c.mx[:, :], in_=src[:, :, :], axis=AX.X, op=ALU.max), reads=[srck], writes=["mx"])
            P.op("dve", lambda e, src=src: e.tensor_tensor(out=c.eq[:, :, :], in0=src[:, :, :], in1=bc(c.mx[:, :]), op=ALU.is_ge), reads=[srck, "mx"], writes=["eq"])
            P.op("dve", lambda e, src=src: e.scalar_tensor_tensor(out=c.g2[:, :, :], in0=c.eq[:, :, :], scalar=-1e30, in1=src[:, :, :], op0=ALU.mult, op1=ALU.add),
                 reads=["eq", srck], writes=["g2"])
            src = c.g2
            srck = "g2"
        P.op("dve", lambda e: e.tensor_reduce(out=c.mx[:, :], in_=c.g2[:, :, :], axis=AX.X, op=ALU.max), reads=["g2"], writes=["mx"])
        P.op("dve", lambda e: e.tensor_tensor(out=c.eq[:, :, :], in0=c.gm[:, :, :], in1=bc(c.mx[:, :]), op=ALU.is_ge), reads=["gm", "mx"], writes=["eq"])
        P.op("dve", lambda e: e.tensor_single_scalar(out=c.g2[:, :, :], in_=c.gm[:, :, :], scalar=-1e29, op=ALU.is_gt), reads=["gm"], writes=["g2"])
        P.op("dve", lambda e: e.tensor_tensor(out=c.eq[:, :, :], in0=c.eq[:, :, :], in1=c.g2[:, :, :], op=ALU.mult), reads=["eq", "g2"], writes=["eq"])
        P.op("dve", lambda e: e.scalar_tensor_tensor(out=c.E[:, :, :], in0=c.eq[:, :, :], scalar=-1.0, in1=c.cm[:, 1, :, :], op0=ALU.add, op1=ALU.mult),
             reads=["eq", "cm"], writes=["E"])
        P.op("dve", lambda e, h=h: e.scalar_tensor_tensor(out=c.E[:, :, :], in0=c.cm[:, 2, :, :], scalar=c.t31[:, h:h + 1], in1=c.E[:, :, :], op0=ALU.mult, op1=ALU.add),
             reads=["E", "cm", "t31"], writes=["E"])
        for qt in range(8):
            pt = c.ps_T[qt // 4]
            P.op("pe", lambda e, qt=qt, pt=pt: e.transpose(pt[0:8, (qt % 4) * 128:(qt % 4 + 1) * 128], c.E[:, qt, :], c.ident[:, :]),
                 reads=["E", "ident"], writes=[("ps_T", qt // 4)])
        for t in range(2):
            P.op("act", lambda e, t=t: e.copy(out=R(c.ET[0:8, t * 512:(t + 1) * 512]), in_=c.ps_T[t][0:8, :]), reads=[("ps_T", t)], writes=["ET"])
        for jb in range(4):
            own = 4 + jb
            qs = slice(jb * 256, (jb + 1) * 256)
            nkt = 2 * (own + 1)
            def emit_qk(kt, sb):
                ps = c.ps_s[sb]
                P.op("pe", lambda e, kt=kt, ps=ps, k=k, q=q, qs=qs: e.matmul(ps[:, 0:256], R(k[:, kt // 2, kt % 2, :]), R(q[:, qs]), start=True, stop=False),
                     reads=kk + [("q", hb)], writes=[("ps_s", sb)])
                P.op("pe", lambda e, n=kt // 2, ps=ps, qs=qs: e.matmul(ps[:, 0:256], R(c.oh[:, n, :]), R(c.ET[:, qs]), start=False, stop=True),
                     reads=["oh", "ET"], writes=[("ps_s", sb)])

            sbs = []
            for kt in range(nkt):
                sbs.append(sidx % 2)
                sidx += 1
            emit_qk(0, sbs[0])
            for kt in range(nkt):
                n = kt // 2
                sb = sbs[kt]
                ps = c.ps_s[sb]
                if kt + 1 < nkt:
                    emit_qk(kt + 1, sbs[kt + 1])
                pb = pidx % 3
                pidx += 1
                PT = c.PT[pb]
                if n >= own - 1:
                    which = (n - (own - 1)) * 2 + kt % 2
                    sS = c.sS[sb]
                    P.op("dve", lambda e, ps=ps, sS=sS, which=which, b=b: e.tensor_tensor(out=sS[:, :], in0=ps[:, 0:256], in1=b[:, which, :], op=ALU.add),
                         reads=[("ps_s", sb), ("b", hb)], writes=[("sS", sb)])
                    P.op("act", lambda e, sS=sS, PT=PT: e.activation(out=R(PT[:, :]), in_=sS[:, :], func=AF.Exp), reads=[("sS", sb)], writes=[("PT", pb)])
                else:
                    P.op("act", lambda e, ps=ps, PT=PT: e.activation(out=R(PT[:, :]), in_=ps[:, 0:256], func=AF.Exp), reads=[("ps_s", sb)], writes=[("PT", pb)])
                P.op("pe", lambda e, kt=kt, PT=PT, v=v, nkt=nkt: e.matmul(c.ps_o[:, 0:256], R(v[:, kt, :]), R(PT[:, :]), start=(kt == 0), stop=(kt == nkt - 1)),
                     reads=vk + [("PT", pb)], writes=["ps_o"])
                P.op("pe", lambda e, kt=kt, PT=PT, nkt=nkt: e.matmul(c.ps_l[:, 0:256], R(c.ones[:, :]), R(PT[:, :]), start=(kt == 0), stop=(kt == nkt - 1)),
                     reads=["ones", ("PT", pb)], writes=["ps_l"])
            P.op("dve", lambda e: e.reciprocal(out=c.rl[:, :], in_=c.ps_l[:, 0:256]), reads=["ps_l"], writes=["rl"])
            P.op("dve", lambda e, h=h, qs=qs: e.tensor_tensor(out=R(c.attnT[:, h, qs]), in0=c.ps_o[:, 0:256], in1=c.rl[:, :], op=ALU.mult),
                 reads=["ps_o", "rl"], writes=[("attnT", h)])
    akeys = [("attnT", h) for h in range(NH)]
    for p in range(2):
        cols = slice(p * TP, (p + 1) * TP)
        for ch in range(KC):
            wb = ch % 2
            wt = c.k[wb]
            P.dma("sp", R(wt[:, :, :, :]), R(wo_d[ch]), writes=[("kp", wb), ("kn", wb)], semkey=("dma", "wo", wb))
            ps = c.ps_s[wb]
            for hh in range(NH):
                P.op("pe", lambda e, hh=hh, ps=ps, wt=wt, cols=cols: e.matmul(ps[:, :], R(wt[:, hh // 2, hh % 2, :]), R(c.attnT[:, hh, cols]), start=(hh == 0), stop=(hh == NH - 1)),
                     reads=[("kp", wb), ("kn", wb)] + akeys, writes=[("ps_s", wb)])
            P.op("dve", lambda e, ps=ps, ch=ch, cols=cols: e.scalar_tensor_tensor(out=c.xT[:, ch, cols], in0=ps[:, :], scalar=c.modT[:, 32 + ch:33 + ch],
                                                                                  in1=c.xT[:, ch, cols], op0=ALU.mult, op1=ALU.add),
                 reads=[("ps_s", wb), "modT", ("xT", p, ch)], writes=[("xT", p, ch)])


def ada_body(P, c, io, groups):
    nch = 48
    P.phase()
    w = [P.alloc([128, KC, 128], r=True) for _ in range(2)]
    cond = P.alloc([128, KC, 2], r=True)
    cT = P.alloc([128, KC, 2])
    bT = P.alloc([128, 4, nch])
    modS = P.alloc([128, 4, nch])
    ps = [c.bank[1], c.bank[2]]
    P.dma("sp", cT[:, :, :], io["cT"], writes=["cT"])
    P.dma("sp", bT[:, :, :], io["badaT"], writes=["bT"])
    P.op("act", lambda e: e.activation(out=R(cond[:, :, :]), in_=cT[:, :, :], func=AF.Silu), reads=["cT"], writes=["cond"])
    t = 0
    for l in range(4):
        for ch in range(nch):
            wb = t % 2
            t += 1
            P.dma("sp", R(w[wb][:, :, :]), R(io["wada"][l, ch]), writes=[("w", wb)])
            for kc in range(KC):
                P.op("pe", lambda e, kc=kc, wb=wb: e.matmul(ps[wb][:, 0:2], R(w[wb][:, kc, :]), R(cond[:, kc, :]), start=(kc == 0), stop=(kc == KC - 1)),
                     reads=[("w", wb), "cond"], writes=[("ps", wb)])
            P.op("dve", lambda e, wb=wb, l=l, ch=ch: e.tensor_scalar(out=modS[:, l, ch:ch + 1], in0=ps[wb][:, 0:1], scalar1=bT[:, l, ch:ch + 1], scalar2=None, op0=ALU.add),
                 reads=[("ps", wb), "bT"], writes=["modS"])
    P.dma("sp", io["adai"].rearrange("p (l c) -> p l c", c=nch), modS[:, :, :], reads=["modS"], writes=["adai"])
    P.cc("AllGather", groups, io["adai"], io["adao"], ["adai"], ["adao"], "cc_ada")
    for r in range(2):
        P.dma("sp", c.modT_all[:, :, r * nch:(r + 1) * nch], io["adao"][r * 128:(r + 1) * 128, :].rearrange("p (l c) -> p l c", c=nch),
              reads=["adao"], writes=["modT"], semkey=("dma", "modT", r))


def build_fused(ncore, stop_after=99):
    nc = new_nc()
    nch = 48
    groups = [[2 * g, 2 * g + 1] for g in range(ncore // 2)]
    dram = lambda n, s, k="ExternalInput": nc.dram_tensor(n, list(s), F32, kind=k).ap()
    io = {}
    io["xT"] = dram("xT", [128, KC, TC])
    io["cT"] = dram("cT", [128, KC, 2])
    io["scal"] = dram("scal", [128, 8])
    io["ic"] = dram("ic", [128, 4, 4, 16])
    io["cmask"] = dram("cmask", [128, 3, 8, 8])
    io["wada"] = dram("wada", [4, nch, 128, KC, 128])
    io["badaT"] = dram("badaT", [128, 4, nch])
    io["vecs"] = dram("vecs", [4, 128, 64])
    io["convw"] = dram("convw", [4, 128, 3, 2 * FC])
    io["convb"] = dram("convb", [4, 128, 2 * FC])
    io["wup"] = dram("wup", [4, FC, 128, KC, 256])
    io["wdn"] = dram("wdn", [4, NP, 2, 128, NF, 1024])
    io["poolw"] = dram("poolw", [2, 4, 128, 8, 256])
    io["wqkv"] = dram("wqkv", [2, 24, 128, KC, 256])
    io["wo"] = dram("wo", [2, KC, 128, 8, 2, 128])
    io["biasT"] = dram("biasT", [NH, 128, 4, 256])
    io["tab31"] = dram("tab31", [128, NH])
    io["oh"] = dram("oh", [128, 8, 128])
    io["zeros"] = dram("zeros", [128, TC])
    io["ident"] = dram("ident", [128, 128])
    io["yo"] = dram("yo", [128, KC, TC], "ExternalOutput")
    io["xhi"] = dram("xhi", [128, KC * HW], "Internal")
    io["xho"] = dram("xho", [256, KC * HW], "Internal")
    io["adai"] = dram("adai", [128, 4 * nch], "Internal")
    io["adao"] = dram("adao", [256, 4 * nch], "Internal")
    io["qo"] = dram("qo", [NH * 128, TC], "Internal")
    io["ko"] = [dram("ko%d" % g, [512, TC], "Internal") for g in range(4)]
    io["vo"] = [dram("vo%d" % g, [256, D], "Internal") for g in range(4)]
    io["kex"] = [dram("kex%d" % g, [1024, TC], "Internal") for g in range(4)]
    io["vex"] = [dram("vex%d" % g, [512, D], "Internal") for g in range(4)]

    P = Prog(nc)
    c = Ctx()
    c.xT = P.sbuf("xT", [128, KC, TC])
    c.xh = P.sbuf("xh", [128, KC, HW])
    c.modT_all = P.sbuf("modT", [128, 4, 96])
    c.scal = P.sbuf("scal", [128, 8])
    P.setup_arenas(32000, 3700)
    c.bank = [P.psum("bank%d" % k, [128, 512]) for k in range(7)]
    P.dma("sp", c.xT[:, :, 0:TP], io["xT"][:, :, 0:TP], writes=[("xT", 0, k) for k in range(KC)], semkey="ldx0")
    P.dma("sp", c.xT[:, :, TP:TC], io["xT"][:, :, TP:TC], writes=[("xT", 1, k) for k in range(KC)], semkey="ldx1")
    P.dma("sp", c.scal[:], io["scal"], writes=["scal"])
    ada_body(P, c, io, groups)
    step = 0
    for i in range(4):
        if step >= stop_after:
            break
        if i % 2 == 0:
            layer_body(P, c, io, i, "pool", False, groups)
            step += 1
        else:
            outk = qkv_body(P, c, io, i)
            step += 1
            if step >= stop_after:
                break
            attn_body(P, c, io, i, outk, groups)
            step += 1
            if step >= stop_after:
                break
            layer_body(P, c, io, i, "none", i == 3, groups)
            step += 1
    if stop_after < 99:
        P.phase()
        for p in range(2):
            P.dma("sp", io["yo"][:, :, p * TP:(p + 1) * TP], c.xT[:, :, p * TP:(p + 1) * TP], reads=[("xT", p, k) for k in range(KC)], writes=[("yout", p)])
    P.emit()
    stats = P.stats
    P.close()
    return nc, stats
def to_fm(a):
    T, Dd = a.shape
    return np.ascontiguousarray(a.T.reshape(Dd // 128, 128, T).transpose(1, 0, 2))


def from_fm(a):
    p, k, T = a.shape
    return np.ascontiguousarray(a.transpose(1, 0, 2).reshape(k * p, T).T)


def vec_fm(v):
    return np.ascontiguousarray(v.reshape(-1, 128).T)


def prep_wup(w):
    w = w.reshape(KC, 128, 2, FC, 128)
    return np.ascontiguousarray(w.transpose(3, 1, 0, 2, 4).reshape(FC, 128, KC, 256))


def prep_wdn(w):
    w = w.reshape(NP, NF, 128, 2, 1024)
    return np.ascontiguousarray(w.transpose(0, 3, 2, 1, 4))


def prep_poolw(w):
    w = w.reshape(4, 4, 128, 512).transpose(0, 2, 1, 3)
    return np.ascontiguousarray(w.reshape(4, 128, 8, 256))


def prep_convw(cw):
    return np.ascontiguousarray(cw.reshape(3, 2 * FC, 128).transpose(2, 0, 1))


def prep_ic(first_half):
    ic = np.zeros((128, 4, 4, 16), np.float32)
    for g, w in enumerate(WINS):
        for t in range(16):
            cnt = min(t + 1, w) if first_half else w
            ic[:, g, :, t] = np.float32(1.0) / np.float32(cnt)
    return ic


def prep_scal(first_half):
    s = np.zeros((128, 8), np.float32)
    s[:, 0] = 0.0 if first_half else 1.0
    return s


def _rel_bucket_np(dist):
    n = np.maximum(dist, 0)
    nf = np.maximum(n, 1).astype(np.float32)
    large = 16 + (np.log(nf / np.float32(16)) / np.float32(math.log(128 / 16)) * np.float32(16)).astype(np.int32)
    large = np.minimum(large, 31)
    return np.where(n < 16, n, large)


def prep_bias(rel_table):
    t = np.arange(128)[:, None, None]
    which = np.arange(4)[None, :, None]
    s = np.arange(256)[None, None, :]
    blk = which // 2
    kl = (which % 2) * 128 + t
    dist = np.where(blk == 1, s - kl, s + 256 - kl)
    idx = _rel_bucket_np(dist)
    out = rel_table[idx]
    out = np.where((dist < 0)[..., None], np.float32(NEGB), out)
    return np.ascontiguousarray(out.transpose(3, 0, 1, 2)).astype(np.float32)


def prep_cmask(first_half):
    cm = np.zeros((128, 3, 8, 8), np.float32)
    for qt in range(8):
        own = 4 + qt // 2
        for n in range(8):
            valid = (n < own) and (n >= 4 or not first_half)
            cm[:, 0, qt, n] = 0.0 if valid else -1e30
            cm[:, 1, qt, n] = 0.0 if n == own else -NEGB
            cm[:, 2, qt, n] = 1.0 if n <= own - 2 else 0.0
    return cm


def prep_wqkv(w):
    return np.ascontiguousarray(w.reshape(KC, 128, 24, 256).transpose(2, 1, 0, 3))


def prep_wo(w):
    return np.ascontiguousarray(w.reshape(NH, 128, KC, 128).transpose(2, 1, 0, 3).reshape(KC, 128, 8, 2, 128))


_FUSED = {}


def _prep_inputs(ncore, x, c, norm_g, w_ada, b_ada, pool_w, pool_scale, w_qkv, w_o, rel_table,
                 w_up, conv_w, conv_b, w_down, final_g):
    nch = 48
    W = nch * 128
    vecs = np.stack([np.concatenate([vec_fm(norm_g[i, 0]), vec_fm(norm_g[i, 1]), vec_fm(pool_scale[i // 2]), vec_fm(final_g)], axis=1)
                     for i in range(4)])
    convw = np.stack([prep_convw(conv_w[i]) for i in range(4)])
    convb = np.stack([vec_fm(conv_b[i]) for i in range(4)])
    wup = np.stack([prep_wup(w_up[i]) for i in range(4)])
    wdn = np.stack([prep_wdn(w_down[i]) for i in range(4)])
    poolw = np.stack([prep_poolw(pool_w[l]) for l in range(2)])
    wqkv = np.stack([prep_wqkv(w_qkv[l]) for l in range(2)])
    wo = np.stack([prep_wo(w_o[l]) for l in range(2)])
    biasT = prep_bias(rel_table)
    tab31 = np.ascontiguousarray(np.broadcast_to(rel_table[31][None, :], (128, NH)))
    oh = np.zeros((128, 8, 128), np.float32)
    for n in range(8):
        oh[n, n, :] = 1.0
    ident = np.eye(128, dtype=np.float32)
    zeros = np.zeros((128, TC), np.float32)
    maps = []
    for j in range(ncore):
        b, half = j // 2, j % 2
        wa = w_ada[:, :, half * W:(half + 1) * W].reshape(4, KC, 128, nch, 128).transpose(0, 3, 2, 1, 4)
        ba = b_ada[:, half * W:(half + 1) * W].reshape(4, nch, 128).transpose(2, 0, 1)
        cb = c[b].reshape(KC, 128).T
        cT = np.ascontiguousarray(np.stack([cb, cb], axis=2))
        maps.append(dict(xT=to_fm(x[b, half * TC:(half + 1) * TC]), cT=cT, scal=prep_scal(half == 0), ic=prep_ic(half == 0),
                         cmask=prep_cmask(half == 0), wada=np.ascontiguousarray(wa), badaT=np.ascontiguousarray(ba),
                         vecs=np.ascontiguousarray(vecs), convw=convw, convb=convb, wup=wup, wdn=wdn, poolw=poolw, wqkv=wqkv, wo=wo,
                         biasT=biasT, tab31=tab31, oh=oh, ident=ident, zeros=zeros))
    return maps


def run_fused(ncore, stop_after=99, **inp):
    if ncore not in _FUSED:
        _FUSED[ncore] = build_fused(ncore, stop_after)[0]
    maps = _prep_inputs(ncore, **inp)
    res = run_bass_kernel_spmd(_FUSED[ncore], maps, core_ids=list(range(ncore)))
    return [r["yo"] for r in res.results]


def kernel(x, c, norm_g, w_ada, b_ada, pool_w, pool_scale, w_qkv, w_o, rel_table,
           w_up, conv_w, conv_b, w_down, final_g):
    f32 = lambda a: np.ascontiguousarray(np.asarray(a, dtype=np.float32))
    ys = run_fused(8, x=f32(x), c=f32(c), norm_g=f32(norm_g), w_ada=f32(w_ada), b_ada=f32(b_ada), pool_w=f32(pool_w),
                   pool_scale=f32(pool_scale), w_qkv=f32(w_qkv), w_o=f32(w_o), rel_table=f32(rel_table), w_up=f32(w_up),
                   conv_w=f32(conv_w), conv_b=f32(conv_b), w_down=f32(w_down), final_g=f32(final_g))
    out = np.zeros((4, 2 * TC, D), np.float32)
    for j in range(8):
        out[j // 2, (j % 2) * TC:(j % 2 + 1) * TC] = from_fm(ys[j])
    return out
```

```python
from contextlib import ExitStack
import math
import numpy as np
import concourse.bass as bass
import concourse.mybir as mybir
from concourse.bass_utils import run_bass_kernel_spmd

F32 = mybir.dt.float32
F32R = mybir.dt.float32r
AF = mybir.ActivationFunctionType
ALU = mybir.AluOpType
AX = mybir.AxisListType

ENGS = ("pe", "act", "dve", "pool", "sp")

D = 2048
KC = 16
FF = 5632
FC = 44
NF = 4
NP = FC // NF
TP = 512
TC = 1024
HW = 32
NH = 16
EPS = 1e-6
WINS = (2, 4, 8, 16)
NEGB = -30000.0


class Prog:
    def __init__(self, nc, same_engine_sync=True):
        self.nc = nc
        self.ops = []
        self.same_engine_sync = same_engine_sync
        self.stack = ExitStack()
        self.arR = self.arF = None

    def sbuf(self, name, shape, dtype=F32):
        return self.stack.enter_context(self.nc.sbuf_tensor("sb_" + name, list(shape), dtype))

    def psum(self, name, shape, dtype=F32):
        return self.stack.enter_context(self.nc.psum_tensor("pp_" + name, list(shape), dtype))

    def setup_arenas(self, words_r, words_f):
        self.capR, self.capF = words_r, words_f
        self.arR = self.sbuf("arenaR", [128, words_r])
        self.arF = self.sbuf("arenaF", [128, words_f])
        self.offR = self.offF = 0

    def phase(self):
        self.offR = self.offF = 0
        self.ops.append(dict(barrier=True))

    def alloc(self, shape, r=False):
        words = 1
        for s_ in shape[1:]:
            words *= s_
        if r:
            off, ar, cap = self.offR, self.arR, self.capR
            self.offR += words
        else:
            off, ar, cap = self.offF, self.arF, self.capF
            self.offF += words
        assert off + words <= cap, ("arena overflow", r, off, words, cap)
        v = ar[0:shape[0], off:off + words]
        if len(shape) == 3:
            v = v.rearrange("p (a b) -> p a b", b=shape[2])
        elif len(shape) == 4:
            v = v.rearrange("p (a b c) -> p a b c", b=shape[2], c=shape[3])
        return v

    def op(self, eng, fn, reads=(), writes=()):
        self.ops.append(dict(eng=eng, fn=fn, reads=tuple(reads), writes=tuple(writes),
                             dma=False, semkey=None))

    def dma(self, q, out, in_, reads=(), writes=(), semkey=None, **kw):
        writes = tuple(writes)
        reads = tuple(reads)
        if semkey is None:
            semkey = ("dma", writes[0])
        self.ops.append(dict(eng=q, fn=lambda e: e.dma_start(out=out, in_=in_, **kw),
                             reads=reads, writes=writes, dma=True, semkey=semkey))

    def cc(self, kind, groups, in_ap, out_ap, reads, writes, semkey):
        self.ops.append(dict(eng="pool", fn=lambda e: e.collective_compute(kind, ALU.bypass, replica_groups=groups, ins=[in_ap], outs=[out_ap]),
                             reads=tuple(reads), writes=tuple(writes), dma=True, semkey=semkey, inc=1))

    def emit(self, final_wait_eng="sp"):
        nc = self.nc
        allops = self.ops
        ops = []
        bar_before = set()
        for o in allops:
            if o.get("barrier"):
                bar_before.add(len(ops))
            else:
                ops.append(o)
        n = len(ops)
        last_w = {}
        readers = {}
        deps = [None] * n
        last_on_eng = {}
        last_dma_key = {}
        pending_bar = {}
        for i, o in enumerate(ops):
            if i in bar_before:
                snap = set(last_on_eng.values()) | set(last_dma_key.values())
                for e in ENGS:
                    pending_bar[e] = set(snap)
            d = set()
            for r in o["reads"]:
                if r in last_w:
                    d.add(last_w[r])
            for w in o["writes"]:
                if w in last_w:
                    d.add(last_w[w])
                d.update(readers.get(w, ()))
            if pending_bar.get(o["eng"]):
                d.update(pending_bar[o["eng"]])
                pending_bar[o["eng"]] = None
            d.discard(i)
            deps[i] = d
            for r in o["reads"]:
                readers.setdefault(r, []).append(i)
            for w in o["writes"]:
                last_w[w] = i
                readers[w] = []
            if o["dma"]:
                last_dma_key[o["semkey"]] = i
            else:
                last_on_eng[o["eng"]] = i
        needed = [False] * n
        for i, o in enumerate(ops):
            best = {}
            pruned = set()
            for j in deps[i]:
                oj = ops[j]
                if oj["dma"]:
                    pruned.add(j)
                elif best.get(oj["eng"], -1) < j:
                    best[oj["eng"]] = j
            pruned.update(best.values())
            deps[i] = pruned
        for i, o in enumerate(ops):
            keep = set()
            for j in deps[i]:
                oj = ops[j]
                if oj["dma"]:
                    keep.add(j)
                    continue
                if oj["eng"] == o["eng"]:
                    if o["dma"]:
                        continue
                    if o["eng"] == "pe" or not self.same_engine_sync:
                        continue
                keep.add(j)
                needed[j] = True
            deps[i] = keep
        eng_sem = {e: self.stack.enter_context(nc.semaphore("s_" + e)) for e in ENGS}
        dma_sem = {}
        dma_cnt = {}
        tok = [None] * n
        eng_cnt = {e: 0 for e in ENGS}
        for i, o in enumerate(ops):
            if o["dma"]:
                k = o["semkey"]
                if k not in dma_sem:
                    dma_sem[k] = self.stack.enter_context(nc.semaphore("d%d" % len(dma_sem)))
                    dma_cnt[k] = 0
                dma_cnt[k] += o.get("inc", 16)
                tok[i] = (dma_sem[k], dma_cnt[k])
            elif needed[i]:
                eng_cnt[o["eng"]] += 1
                tok[i] = (eng_sem[o["eng"]], eng_cnt[o["eng"]])
        self.stats = dict(n_ops=n, n_dma_sems=len(dma_sem), eng_cnt=dict(eng_cnt), max_dma=max(dma_cnt.values()) if dma_cnt else 0)
        per_eng = {e: [] for e in ENGS}
        for i, o in enumerate(ops):
            per_eng[o["eng"]].append(i)
        block = self.stack.enter_context(nc.Block())

        def run(e_name, eng):
            waited = {}
            for i in per_eng[e_name]:
                o = ops[i]
                ws = {}
                for j in deps[i]:
                    s, v = tok[j]
                    key = id(s)
                    if key not in ws or ws[key][1] < v:
                        ws[key] = (s, v)
                for key, (s, v) in ws.items():
                    if waited.get(key, 0) >= v:
                        continue
                    eng.wait_ge(s, v)
                    waited[key] = v
                ins = o["fn"](eng)
                if tok[i] is not None:
                    ins.then_inc(tok[i][0], o.get("inc", 16) if o["dma"] else 1)
            if e_name == final_wait_eng:
                for k, s in dma_sem.items():
                    if waited.get(id(s), 0) < dma_cnt[k]:
                        eng.wait_ge(s, dma_cnt[k])

        @block.sync
        def _(eng):
            run("sp", eng)

        @block.scalar
        def _(eng):
            run("act", eng)

        @block.vector
        def _(eng):
            run("dve", eng)

        @block.gpsimd
        def _(eng):
            run("pool", eng)

        @block.tensor
        def _(eng):
            run("pe", eng)

    def close(self):
        self.stack.close()


def new_nc():
    nc = bass.Bass("TRN2", target_bir_lowering=False)
    nc.dge_precook = False
    return nc


def R(ap):
    return ap.bitcast(F32R)


class Ctx:
    def fork(self):
        n = Ctx()
        n.__dict__.update(self.__dict__)
        return n


def emit_norm(P, c, xsrc, xkeyf, col0, width, hdst0, A, Bv, tag, mask=None):
    W = width
    ssq = c.ps_n
    ring = c.sqring
    for kc in range(KC):
        sq, sk = ring[kc % len(ring)]
        P.op("act", lambda e, kc=kc, sq=sq: e.activation(out=R(sq[:, 0:W]), in_=xsrc[:, kc, col0:col0 + W], func=AF.Square),
             reads=[xkeyf(kc)], writes=[sk])
        P.op("pe", lambda e, kc=kc, sq=sq: e.matmul(ssq[:, 0:W], R(c.ones[:, :]), R(sq[:, 0:W]), start=(kc == 0), stop=(kc == KC - 1)),
             reads=[sk, "ones"], writes=["ps_n"])
    P.op("dve", lambda e: e.tensor_scalar(out=c.rstd[:, 0:W], in0=ssq[:, 0:W], scalar1=1.0 / D, scalar2=EPS, op0=ALU.mult, op1=ALU.add),
         reads=["ps_n"], writes=["rstd"])
    P.op("act", lambda e: e.activation(out=c.rstd[:, 0:W], in_=c.rstd[:, 0:W], func=AF.Sqrt), reads=["rstd"], writes=["rstd"])
    P.op("dve", lambda e: e.reciprocal(out=c.rstd[:, 0:W], in_=c.rstd[:, 0:W]), reads=["rstd"], writes=["rstd"])
    if mask is not None:
        P.op("dve", lambda e: e.tensor_scalar(out=c.rstd[:, 0:W], in0=c.rstd[:, 0:W], scalar1=mask, scalar2=None, op0=ALU.mult),
             reads=["rstd", "scal"], writes=["rstd"])
    for kc in range(KC):
        tb = "tbuf%d" % (kc % 2)
        tt = c.ntmp[kc % 2]
        P.op("pool" if kc % 2 == 0 else "dve", lambda e, kc=kc, tt=tt: e.tensor_tensor(out=tt[:, 0:W], in0=xsrc[:, kc, col0:col0 + W], in1=c.rstd[:, 0:W], op=ALU.mult),
             reads=[xkeyf(kc), "rstd"], writes=[tb])
        if mask is None:
            P.op("act", lambda e, kc=kc, tt=tt: e.activation(out=R(c.hT[:, kc, hdst0:hdst0 + W]), in_=tt[:, 0:W], func=AF.Identity,
                                                             bias=Bv[:, kc:kc + 1], scale=A[:, kc:kc + 1]),
                 reads=[tb, tag], writes=[("hT", kc)])
        else:
            P.op("act", lambda e, kc=kc, tt=tt: e.activation(out=tt[:, 0:W], in_=tt[:, 0:W], func=AF.Identity,
                                                             bias=Bv[:, kc:kc + 1], scale=A[:, kc:kc + 1]),
                 reads=[tb, tag], writes=[tb])
            P.op("dve", lambda e, kc=kc, tt=tt: e.tensor_scalar(out=R(c.hT[:, kc, hdst0:hdst0 + W]), in0=tt[:, 0:W], scalar1=mask, scalar2=None, op0=ALU.mult),
                 reads=[tb, "scal"], writes=[("hT", kc)])


def emit_ffn_pass(P, c, p, ffn_w):
    wup_d, wdn_d = ffn_w
    cols = slice(p * TP, (p + 1) * TP)
    hkeys = [("hT", kc) for kc in range(KC)]

    def up_chunk(fc):
        wb = fc % 2
        wt = c.wup[wb]
        P.dma("sp", R(wt[:]), R(wup_d[fc]), writes=[("wup", wb)])
        for half, (pm, ph) in enumerate(((c.ps_v, c.ps_hv), (c.ps_g, c.ps_hg))):
            pk = "ps_v" if half == 0 else "ps_g"
            phk = "ps_hv" if half == 0 else "ps_hg"
            for kc in range(KC):
                P.op("pe", lambda e, kc=kc, pm=pm, half=half, wt=wt: e.matmul(pm[:, :], R(wt[:, kc, half * 128:(half + 1) * 128]),
                                                                              R(c.hT[:, kc, HW:HW + TP]), start=(kc == 0), stop=(kc == KC - 1)),
                     reads=[("wup", wb)] + hkeys, writes=[pk])
            ub = c.ubuf[half]
            uk = "ubuf%d" % half
            j = fc + half * FC
            if p == 0:
                for kc in range(KC):
                    P.op("pe", lambda e, kc=kc, ph=ph, half=half, wt=wt: e.matmul(ph[:, 0:2], R(wt[:, kc, half * 128:(half + 1) * 128]),
                                                                                  R(c.hT[:, kc, HW - 2:HW]), start=(kc == 0), stop=(kc == KC - 1)),
                         reads=[("wup", wb)] + hkeys, writes=[phk])
            P.op("act", lambda e, ub=ub, pm=pm: e.copy(out=ub[:, 2:2 + TP], in_=pm[:, :]), reads=[pk], writes=[uk])
            if p == 0:
                P.op("act", lambda e, ub=ub, ph=ph: e.copy(out=ub[:, 0:2], in_=ph[:, 0:2]), reads=[phk], writes=[uk + "h"])
                P.op("pool", lambda e, ub=ub, j=j: e.tensor_copy(out=c.usave[:, j, :], in_=ub[:, TP:TP + 2]), reads=[uk], writes=[("usave", j)])
            else:
                P.op("pool", lambda e, ub=ub, j=j: e.tensor_copy(out=ub[:, 0:2], in_=c.usave[:, j, :]), reads=[("usave", j)], writes=[uk + "h"])
            tt = c.tbuf[half]
            tk = "tbuf%d" % half
            P.op("act", lambda e, pm=pm, tt=tt, j=j: e.activation(out=tt[:, :], in_=pm[:, :], func=AF.Identity,
                                                                bias=c.convb[:, j:j + 1], scale=c.convw[:, 2, j:j + 1]),
                 reads=[pk, "convw"], writes=[tk])
            P.op("dve", lambda e, ub=ub, tt=tt, j=j: e.scalar_tensor_tensor(out=tt[:, :], in0=ub[:, 1:1 + TP], scalar=c.convw[:, 1, j:j + 1],
                                                                            in1=tt[:, :], op0=ALU.mult, op1=ALU.add),
                 reads=[uk, uk + "h", tk, "convw"], writes=[tk])
            P.op("dve", lambda e, ub=ub, tt=tt, j=j: e.scalar_tensor_tensor(out=tt[:, :], in0=ub[:, 0:TP], scalar=c.convw[:, 0, j:j + 1],
                                                                            in1=tt[:, :], op0=ALU.mult, op1=ALU.add),
                 reads=[uk, uk + "h", tk, "convw"], writes=[tk])
        P.op("act", lambda e: e.activation(out=R(c.sil[:, :]), in_=c.tbuf[1][:, :], func=AF.Silu), reads=["tbuf1"], writes=["sil"])
        ab = fc % (2 * NF)
        P.op("pool", lambda e, ab=ab: e.tensor_tensor(out=R(c.aT[:, ab, :]), in0=c.sil[:, :], in1=c.tbuf[0][:, :], op=ALU.mult),
             reads=["sil", "tbuf0"], writes=[("aT", ab)])

    def down_piece(pc):
        for dh in range(2):
            wb = dh
            wt = c.wdn[wb]
            P.dma("sp", R(wt[:]), R(wdn_d[pc, dh]), writes=[("wdn", wb)])
            for dl in range(8):
                dch = dh * 8 + dl
                pb = dch % 2
                pd = c.ps_d[pb]
                for f in range(NF):
                    ab = (pc * NF + f) % (2 * NF)
                    P.op("pe", lambda e, f=f, ab=ab, pd=pd, dl=dl, wt=wt: e.matmul(pd[:, :], R(wt[:, f, dl * 128:(dl + 1) * 128]), R(c.aT[:, ab, :]),
                                                                                    start=(f == 0), stop=(f == NF - 1)),
                         reads=[("wdn", wb), ("aT", ab)], writes=[("ps_d", pb)])
                if dch % 2 == 0:
                    P.op("dve", lambda e, pd=pd, dch=dch: e.scalar_tensor_tensor(out=c.xT[:, dch, cols], in0=pd[:, :], scalar=c.modT[:, 80 + dch:81 + dch],
                                                                                  in1=c.xT[:, dch, cols], op0=ALU.mult, op1=ALU.add),
                         reads=[("ps_d", pb), "modT", ("xT", p, dch)], writes=[("xT", p, dch)])
                else:
                    P.op("act", lambda e, pd=pd, dch=dch: e.activation(out=c.dtmp[:, :], in_=pd[:, :], func=AF.Copy, scale=c.modT[:, 80 + dch:81 + dch]),
                         reads=[("ps_d", pb), "modT"], writes=["rstd"])
                    P.op("pool", lambda e, dch=dch: e.tensor_tensor(out=c.xT[:, dch, cols], in0=c.dtmp[:, :], in1=c.xT[:, dch, cols], op=ALU.add),
                         reads=["rstd", ("xT", p, dch)], writes=[("xT", p, dch)])

    for pc in range(NP):
        for f in range(NF):
            up_chunk(pc * NF + f)
        if pc >= 1:
            down_piece(pc - 1)
    down_piece(NP - 1)


def emit_final_norm(P, c, p, ydst):
    emit_norm(P, c, c.xT, (lambda kc, p=p: ("xT", p, kc)), p * TP, TP, HW, c.vecs[:, 48:64], c.zeros16, "vecs")
    P.dma("sp", ydst[:, :, p * TP:(p + 1) * TP], c.hT[:, :, HW:HW + TP], reads=[("hT", kc) for kc in range(KC)], writes=[("yout", p)])


def emit_ones(P, c):
    P.op("pool", lambda e: e.memset(c.ones0[:, :], 1.0), writes=["ones0"])
    P.op("dve", lambda e: e.tensor_copy(out=R(c.ones[:, :]), in_=c.ones0[:, :]), reads=["ones0"], writes=["ones"])


def halo_exchange(P, c, io, groups):
    xk = [("xT", 1, k) for k in range(KC)]
    P.dma("sp", io["xhi"].rearrange("p (k t) -> p k t", t=HW), c.xT[:, :, TC - HW:TC], reads=xk, writes=["xhi"])
    P.cc("AllGather", groups, io["xhi"], io["xho"], ["xhi"], ["xho"], "cc_xh")
    P.dma("sp", c.xh[:, :, :], io["xho"][0:128, :].rearrange("p (k t) -> p k t", t=HW), reads=["xho"], writes=["xh"])


def layer_body(P, c, io, i, mixer, final, groups):
    li = i // 2
    c = c.fork()
    P.phase()
    c.hT = P.alloc([128, KC, HW + TP], r=True)
    c.wup = [P.alloc([128, KC, 256], r=True) for _ in range(2)]
    c.wdn = [P.alloc([128, NF, 1024], r=True) for _ in range(2)]
    c.aT = P.alloc([128, 2 * NF, TP], r=True)
    c.dT = P.alloc([128, 4, HW + TP], r=True)
    c.sil = P.alloc([128, TP], r=True)
    c.sqring = [(c.aT[:, j, :], ("aT", j)) for j in range(2 * NF)]
    c.ones = P.alloc([128, 128], r=True)
    c.ubuf = [P.alloc([128, 2 + TP]) for _ in range(2)]
    c.tbuf = [P.alloc([128, TP]) for _ in range(2)]
    c.ntmp = c.tbuf
    c.rstd = P.alloc([128, TP])
    c.dtmp = c.rstd
    c.fix = P.alloc([128, 4, 16])
    c.usave = P.alloc([128, 2 * FC, 2])
    c.convw = P.alloc([128, 3, 2 * FC])
    c.convb = P.alloc([128, 2 * FC])
    c.ic = P.alloc([128, 4, 4, 16])
    c.ones0 = P.alloc([128, 128])
    c.AB = P.alloc([128, 4, 16])
    c.zeros16 = P.alloc([128, 16])
    c.vecs = P.alloc([128, 64])
    c.pw = [c.wup[k][:, 0:8, :] for k in range(2)]
    c.sA = c.wdn[0][:, :, 0:HW + TP]
    c.sB = c.wdn[1][:, :, 0:HW + TP]
    c.ps_n, c.ps_v, c.ps_g, c.ps_hv, c.ps_hg = c.bank[0], c.bank[1], c.bank[2], c.bank[3], c.bank[4]
    c.ps_d = [c.bank[5], c.bank[6]]
    c.ps_p = c.ps_d
    c.ps_ph = c.ps_hv
    c.modT = c.modT_all[:, i, :]
    hmask = c.scal[:, 0:1]

    P.dma("sp", c.vecs[:, :], io["vecs"][i], writes=["vecs"])
    P.dma("sp", c.convw[:, :, :], io["convw"][i], writes=["convw"])
    P.dma("sp", c.convb[:, :], io["convb"][i], writes=["convw"], semkey=("dma", "convb"))
    emit_ones(P, c)
    P.op("pool", lambda e: e.memset(c.zeros16[:, :], 0.0), writes=["zeros16"])
    P.op("dve", lambda e: e.scalar_tensor_tensor(out=c.AB[:, 0, :], in0=c.modT[:, 16:32], scalar=1.0, in1=c.vecs[:, 0:16], op0=ALU.add, op1=ALU.mult),
         reads=["modT", "vecs"], writes=["AB"])
    P.op("dve", lambda e: e.scalar_tensor_tensor(out=c.AB[:, 1, :], in0=c.modT[:, 64:80], scalar=1.0, in1=c.vecs[:, 16:32], op0=ALU.add, op1=ALU.mult),
         reads=["modT", "vecs"], writes=["AB"])
    P.op("dve", lambda e: e.tensor_tensor(out=c.AB[:, 2, :], in0=c.modT[:, 32:48], in1=c.vecs[:, 32:48], op=ALU.mult),
         reads=["modT", "vecs"], writes=["AB"])
    halo_exchange(P, c, io, groups)

    if mixer == "pool":
        pw_d = io["poolw"][li]
        P.dma("sp", c.ic[:, :, :, :], io["ic"], writes=["ic"])
        for p in range(2):
            if p == 0:
                emit_norm(P, c, c.xh, (lambda kc: "xh"), 0, HW, 0, c.AB[:, 0, :], c.modT[:, 0:16], "AB", mask=hmask)
            else:
                for kc in range(KC):
                    P.op("pool", lambda e, kc=kc: e.tensor_copy(out=R(c.hT[:, kc, 0:HW]), in_=R(c.hT[:, kc, TP:TP + HW])),
                         reads=[("hT", kc)], writes=[("hT", kc)])
            emit_norm(P, c, c.xT, (lambda kc, p=p: ("xT", p, kc)), p * TP, TP, HW, c.AB[:, 0, :], c.modT[:, 0:16], "AB")
            WT = HW + TP
            for g in range(4):
                hg = c.hT[:, 4 * g:4 * g + 4, :]
                hk = [("hT", 4 * g + k) for k in range(4)]
                bufs = [c.sA, c.sB]
                src, srck = hg, hk
                sh = 1
                for st in range(g + 1):
                    dst = bufs[st % 2]
                    dk = ("wdn", st % 2)
                    P.op("pool", lambda e, dst=dst, src=src, sh=sh: e.tensor_tensor(out=R(dst[:, :, sh:WT]), in0=src[:, :, sh:WT], in1=src[:, :, 0:WT - sh], op=ALU.add),
                         reads=srck, writes=[dk])
                    P.op("pool", lambda e, dst=dst, src=src, sh=sh: e.tensor_copy(out=R(dst[:, :, 0:sh]), in_=src[:, :, 0:sh]),
                         reads=srck, writes=[dk])
                    src, srck = dst, [dk]
                    sh *= 2
                P.op("dve", lambda e, src=src, g=g, hg=hg: e.scalar_tensor_tensor(out=R(c.dT[:, :, :]), in0=src[:, :, :], scalar=1.0 / WINS[g],
                                                                                 in1=hg, op0=ALU.mult, op1=ALU.subtract),
                     reads=srck + hk, writes=["dT"])
                if p == 0:
                    P.op("dve", lambda e, src=src, g=g: e.tensor_tensor(out=c.fix[:, :, :], in0=src[:, :, HW:HW + 16], in1=c.ic[:, g, :, :], op=ALU.mult),
                         reads=srck + ["ic"], writes=["fix"])
                    P.op("dve", lambda e, hg=hg: e.tensor_tensor(out=R(c.dT[:, :, HW:HW + 16]), in0=c.fix[:, :, :], in1=hg[:, :, HW:HW + 16], op=ALU.subtract),
                         reads=["fix", "dT"] + hk, writes=["dT"])
                wb = g % 2
                P.dma("sp", R(c.pw[wb]), R(pw_d[g]), writes=[("wup", wb)])
                for m in range(4):
                    ch = 4 * g + m
                    pb = ch % 2
                    pd = c.ps_p[pb]
                    for kc in range(4):
                        P.op("pe", lambda e, kc=kc, m=m, pd=pd, wb=wb: e.matmul(pd[:, :], R(c.pw[wb][:, kc * 2 + m // 2, (m % 2) * 128:(m % 2) * 128 + 128]), R(c.dT[:, kc, HW:WT]),
                                                                                start=(kc == 0), stop=(kc == 3)),
                             reads=[("wup", wb), "dT"], writes=[("ps_d", pb)])
                    P.op("dve", lambda e, pd=pd, ch=ch, p=p: e.scalar_tensor_tensor(out=c.xT[:, ch, p * TP:(p + 1) * TP], in0=pd[:, :], scalar=c.AB[:, 2, ch:ch + 1],
                                                                                    in1=c.xT[:, ch, p * TP:(p + 1) * TP], op0=ALU.mult, op1=ALU.add),
                         reads=[("ps_d", pb), "AB", ("xT", p, ch)], writes=[("xT", p, ch)])
                    if p == 0:
                        for kc in range(4):
                            P.op("pe", lambda e, kc=kc, m=m, wb=wb: e.matmul(c.ps_ph[:, 0:HW], R(c.pw[wb][:, kc * 2 + m // 2, (m % 2) * 128:(m % 2) * 128 + 128]), R(c.dT[:, kc, 0:HW]),
                                                                            start=(kc == 0), stop=(kc == 3)),
                                 reads=[("wup", wb), "dT"], writes=["ps_hv"])
                        P.op("dve", lambda e, ch=ch: e.scalar_tensor_tensor(out=c.xh[:, ch, :], in0=c.ps_ph[:, 0:HW], scalar=c.AB[:, 2, ch:ch + 1],
                                                                            in1=c.xh[:, ch, :], op0=ALU.mult, op1=ALU.add),
                             reads=["ps_hv", "AB", "xh"], writes=["xh"])

    for p in range(2):
        if p == 0:
            emit_norm(P, c, c.xh, (lambda kc: "xh"), 0, HW, 0, c.AB[:, 1, :], c.modT[:, 48:64], "AB", mask=hmask)
        emit_norm(P, c, c.xT, (lambda kc, p=p: ("xT", p, kc)), p * TP, TP, HW, c.AB[:, 1, :], c.modT[:, 48:64], "AB")
        emit_ffn_pass(P, c, p, (io["wup"][i], io["wdn"][i]))
    if final:
        for p in range(2):
            emit_final_norm(P, c, p, io["yo"])


def qkv_body(P, c, io, i, groups):
    li = i // 2
    c = c.fork()
    P.phase()
    hT2 = [P.alloc([128, KC, HW + TP], r=True) for _ in range(2)]
    c.w = [P.alloc([128, KC, 256], r=True) for _ in range(3)]
    sqr = P.alloc([128, 4, TP], r=True)
    c.sqring = [(sqr[:, j, :], ("sqr", j)) for j in range(4)]
    c.ones = P.alloc([128, 128], r=True)
    c.tbuf = [P.alloc([128, TP]) for _ in range(2)]
    c.ntmp = c.tbuf
    c.rstd = P.alloc([128, TP])
    c.st = [P.alloc([128, 512]) for _ in range(2)]
    c.ones0 = P.alloc([128, 128])
    c.AB = P.alloc([128, 4, 16])
    c.vecs = P.alloc([128, 64])
    c.ps_n = c.bank[0]
    c.ps = [c.bank[1], c.bank[2], c.bank[3]]
    c.modT = c.modT_all[:, i, :]
    w_d = io["wqkv"][li]
    P.dma("sp", c.vecs[:, :], io["vecs"][i], writes=["vecs"])
    emit_ones(P, c)
    P.op("dve", lambda e: e.scalar_tensor_tensor(out=c.AB[:, 0, :], in0=c.modT[:, 16:32], scalar=1.0, in1=c.vecs[:, 0:16], op0=ALU.add, op1=ALU.mult),
         reads=["modT", "vecs"], writes=["AB"])
    for p in range(2):
        cp = c.fork()
        cp.hT = hT2[p]
        emit_norm(P, cp, c.xT, (lambda kc, p=p: ("xT", p, kc)), p * TP, TP, HW, c.AB[:, 0, :], c.modT[:, 0:16], "AB")
    hkeys = [("hT", kc) for kc in range(KC)]
    cnt = 0
    outk = []
    scale_q = float(128 ** -0.5)
    for hg in range(24):
        wb = hg % 3
        wt = c.w[wb]
        P.dma("sp", R(wt[:, :, :]), R(w_d[hg]), writes=[("w", wb)])
        for p in range(2):
            hT = hT2[p]
            for m in range(2 if hg < 16 else 4):
                pb = cnt % 3
                sb = cnt % 2
                cnt += 1
                ps = c.ps[pb]
                st = c.st[sb]
                for kc in range(KC):
                    if hg < 16:
                        P.op("pe", lambda e, kc=kc, m=m, ps=ps, wt=wt, hT=hT: e.matmul(ps[:, :], R(wt[:, kc, m * 128:(m + 1) * 128]), R(hT[:, kc, HW:HW + TP]),
                                                                                       start=(kc == 0), stop=(kc == KC - 1)),
                             reads=[("w", wb)] + hkeys, writes=[("ps", pb)])
                    else:
                        P.op("pe", lambda e, kc=kc, m=m, ps=ps, wt=wt, hT=hT: e.matmul(ps[:, 0:256], R(hT[:, kc, HW + m * 128:HW + (m + 1) * 128]), R(wt[:, kc, :]),
                                                                                       start=(kc == 0), stop=(kc == KC - 1)),
                             reads=[("w", wb)] + hkeys, writes=[("ps", pb)])
                if hg < 8:
                    hd = hg * 2 + m
                    P.op("act", lambda e, ps=ps, st=st: e.activation(out=st[:, :], in_=ps[:, :], func=AF.Copy, scale=scale_q), reads=[("ps", pb)], writes=[("st", sb)])
                    dst = io["qo"][hd * 128:(hd + 1) * 128, p * TP:(p + 1) * TP]
                    key = ("qo", hd, p)
                    src = st[:, :]
                elif hg < 16:
                    hd = (hg - 8) * 2 + m
                    P.op("act", lambda e, ps=ps, st=st: e.copy(out=st[:, :], in_=ps[:, :]), reads=[("ps", pb)], writes=[("st", sb)])
                    dst = io["ko"][hd // 4][(hd % 4) * 128:(hd % 4 + 1) * 128, p * TP:(p + 1) * TP]
                    key = ("ko", hd, p)
                    src = st[:, :]
                else:
                    tt = p * 4 + m
                    P.op("act", lambda e, ps=ps, st=st: e.copy(out=st[:, 0:256], in_=ps[:, 0:256]), reads=[("ps", pb)], writes=[("st", sb)])
                    vg = (hg - 16) // 2
                    dst = io["vo"][vg][tt * 128:(tt + 1) * 128, ((hg - 16) % 2) * 256:((hg - 16) % 2 + 1) * 256]
                    key = ("vo", tt, hg)
                    src = st[:, 0:256]
                outk.append(key)
                P.dma("sp", dst, src, reads=[("st", sb)], writes=[key], semkey=("o", sb))
        if 8 <= hg < 16 and hg % 2 == 1:
            g = (hg - 8) // 2
            P.cc("AllGather", groups, io["ko"][g], io["kex"][g], [k for k in outk if k[0] == "ko" and k[1] // 4 == g], [("kex", g)], ("cc_k", g))
        if hg >= 16 and hg % 2 == 1:
            g = (hg - 16) // 2
            P.cc("AllGather", groups, io["vo"][g], io["vex"][g], [k for k in outk if k[0] == "vo" and k[2] // 2 - 8 == g], [("vex", g)], ("cc_v", g))
    return outk


def attn_body(P, c, io, i, outk, groups):
    li = i // 2
    kkeys = [k for k in outk if k[0] == "ko"]
    vkeys = [k for k in outk if k[0] == "vo"]
    qkeys = [k for k in outk if k[0] == "qo"]
    c = c.fork()
    P.phase()
    c.attnT = P.alloc([128, NH, TC], r=True)
    c.q = [P.alloc([128, TC], r=True) for _ in range(2)]
    c.k = [P.alloc([128, 8, 2, 128], r=True) for _ in range(2)]
    c.v = [P.alloc([128, 16, 128], r=True) for _ in range(2)]
    c.PT = [P.alloc([128, 256], r=True) for _ in range(3)]
    c.ET = P.alloc([128, TC], r=True)
    c.oh = P.alloc([128, 8, 128], r=True)
    c.kmR = P.alloc([128, 8], r=True)
    c.ones = P.alloc([128, 128], r=True)
    c.b = [P.alloc([128, 4, 256]) for _ in range(2)]
    c.km = P.alloc([128, 8])
    c.gm = P.alloc([128, 8, 8])
    c.g2 = P.alloc([128, 8, 8])
    c.eq = P.alloc([128, 8, 8])
    c.E = P.alloc([128, 8, 8])
    c.mx = P.alloc([128, 8])
    c.sS = [P.alloc([128, 256]) for _ in range(2)]
    c.rl = P.alloc([128, 256])
    c.cm = P.alloc([128, 3, 8, 8])
    c.t31 = P.alloc([128, NH])
    c.ident = P.alloc([128, 128])
    c.ones0 = P.alloc([128, 128])
    c.ps_gate = c.bank[0]
    c.ps_T = [c.bank[1], c.bank[2]]
    c.ps_s = [c.bank[3], c.bank[4]]
    c.ps_o = c.bank[5]
    c.ps_l = c.bank[6]
    c.modT = c.modT_all[:, i, :]
    q_d, b_d, wo_d = io["qo"], io["biasT"], io["wo"][li]
    P.dma("sp", c.cm[:, :, :, :], io["cmask"], writes=["cm"])
    P.dma("sp", c.t31[:, :], io["tab31"], writes=["t31"])
    P.dma("sp", R(c.oh[:, :, :]), R(io["oh"]), writes=["oh"])
    P.dma("sp", R(c.ET[:, :]), R(io["zeros"]), writes=["ET"])
    P.dma("sp", c.ident[:, :], io["ident"], writes=["ident"])
    emit_ones(P, c)

    def bc(ap2):
        return ap2.unsqueeze(2).to_broadcast([128, 8, 8])

    sidx = 0
    pidx = 0
    for h in range(NH):
        hb = h % 2
        q, k, v, b = c.q[hb], c.k[hb], c.v[hb], c.b[hb]
        hr = slice(h * 128, (h + 1) * 128)
        hc = slice(h * 128, (h + 1) * 128)
        P.dma("sp", R(q[:, :]), R(q_d[hr, :]), reads=qkeys, writes=[("q", hb)])
        hg = h // 4
        hq = slice((h % 4) * 128, (h % 4 + 1) * 128)
        P.dma("sp", R(k[:, 0:4, :, :]), R(io["kex"][hg][hq, :].rearrange("p (a b c) -> p a b c", b=2, c=128)), reads=[("kex", hg)], writes=[("kp", hb)])
        P.dma("sp", R(k[:, 4:8, :, :]), R(io["ko"][hg][hq, :].rearrange("p (a b c) -> p a b c", b=2, c=128)), reads=kkeys, writes=[("kn", hb)])
        P.dma("sp", R(v[:, 0:8, :]), R(io["vex"][hg][0:TC, hq].rearrange("(a t) d -> t a d", t=128)), reads=[("vex", hg)], writes=[("vp", hb)])
        P.dma("sp", R(v[:, 8:16, :]), R(io["vo"][hg][:, hq].rearrange("(a t) d -> t a d", t=128)), reads=vkeys, writes=[("vn", hb)])
        P.dma("sp", b[:, :, :], b_d[h], writes=[("b", hb)])
        kk = [("kp", hb), ("kn", hb)]
        vk = [("vp", hb), ("vn", hb)]
        P.op("dve", lambda e, k=k: e.tensor_reduce(out=c.km[:, :], in_=k[:, :, :, :], axis=AX.XY, op=ALU.add), reads=kk, writes=["km"])
        P.op("dve", lambda e: e.tensor_scalar(out=R(c.kmR[:, :]), in0=c.km[:, :], scalar1=1.0 / 256.0, scalar2=None, op0=ALU.mult), reads=["km"], writes=["kmR"])
        for qt in range(8):
            P.op("pe", lambda e, qt=qt, q=q: e.matmul(c.ps_gate[:, qt * 8:(qt + 1) * 8], R(q[:, qt * 128:(qt + 1) * 128]), R(c.kmR[:, :]), start=True, stop=True),
                 reads=[("q", hb), "kmR"], writes=["ps_gate"])
        gate3 = c.ps_gate[:, 0:64].rearrange("p (a b) -> p a b", b=8)
        P.op("dve", lambda e, gate3=gate3: e.tensor_tensor(out=c.gm[:, :, :], in0=gate3, in1=c.cm[:, 0, :, :], op=ALU.add), reads=["ps_gate", "cm"], writes=["gm"])
        src = c.gm
        srck = "gm"
        for it in range(2):
            P.op("dve", lambda e, src=src: e.tensor_reduce(out=c.mx[:, :], in_=src[:, :, :], axis=AX.X, op=ALU.max), reads=[srck], writes=["mx"])
            P.op("dve", lambda e, src=src: e.tensor_tensor(out=c.eq[:, :, :], in0=src[:, :, :], in1=bc(c.mx[:, :]), op=ALU.is_ge), reads=[srck, "mx"], writes=["eq"])
            P.op("dve", lambda e, src=src: e.scalar_tensor_tensor(out=c.g2[:, :, :], in0=c.eq[:, :, :], scalar=-1e30, in1=src[:, :, :], op0=ALU.mult, op1=ALU.add),
                 reads=["eq", srck], writes=["g2"])
            src = c.g2
            srck = "g2"
        P.op("dve", lambda e: e.tensor_reduce(out=c.mx[:, :], in_=c.g2[:, :, :], axis=AX.X, op=ALU.max), reads=["g2"], writes=["mx"])
        P.op("dve", lambda e: e.tensor_tensor(out=c.eq[:, :, :], in0=c.gm[:, :, :], in1=bc(c.mx[:, :]), op=ALU.is_ge), reads=["gm", "mx"], writes=["eq"])
        P.op("dve", lambda e: e.tensor_single_scalar(out=c.g2[:, :, :], in_=c.gm[:, :, :], scalar=-1e29, op=ALU.is_gt), reads=["gm"], writes=["g2"])
        P.op("dve", lambda e: e.tensor_tensor(out=c.eq[:, :, :], in0=c.eq[:, :, :], in1=c.g2[:, :, :], op=ALU.mult), reads=["eq", "g2"], writes=["eq"])
        P.op("dve", lambda e: e.scalar_tensor_tensor(out=c.E[:, :, :], in0=c.eq[:, :, :], scalar=-1.0, in1=c.cm[:, 1, :, :], op0=ALU.add, op1=ALU.mult),
             reads=["eq", "cm"], writes=["E"])
        P.op("dve", lambda e, h=h: e.scalar_tensor_tensor(out=c.E[:, :, :], in0=c.cm[:, 2, :, :], scalar=c.t31[:, h:h + 1], in1=c.E[:, :, :], op0=ALU.mult, op1=ALU.add),
             reads=["E", "cm", "t31"], writes=["E"])
        for qt in range(8):
            pt = c.ps_T[qt // 4]
            P.op("pe", lambda e, qt=qt, pt=pt: e.transpose(pt[0:8, (qt % 4) * 128:(qt % 4 + 1) * 128], c.E[:, qt, :], c.ident[:, :]),
                 reads=["E", "ident"], writes=[("ps_T", qt // 4)])
        for t in range(2):
            P.op("act", lambda e, t=t: e.copy(out=R(c.ET[0:8, t * 512:(t + 1) * 512]), in_=c.ps_T[t][0:8, :]), reads=[("ps_T", t)], writes=["ET"])
        for jb in range(4):
            own = 4 + jb
            qs = slice(jb * 256, (jb + 1) * 256)
            nkt = 2 * (own + 1)
            def emit_qk(kt, sb):
                ps = c.ps_s[sb]
                P.op("pe", lambda e, kt=kt, ps=ps, k=k, q=q, qs=qs: e.matmul(ps[:, 0:256], R(k[:, kt // 2, kt % 2, :]), R(q[:, qs]), start=True, stop=False),
                     reads=kk + [("q", hb)], writes=[("ps_s", sb)])
                P.op("pe", lambda e, n=kt // 2, ps=ps, qs=qs: e.matmul(ps[:, 0:256], R(c.oh[:, n, :]), R(c.ET[:, qs]), start=False, stop=True),
                     reads=["oh", "ET"], writes=[("ps_s", sb)])

            sbs = []
            for kt in range(nkt):
                sbs.append(sidx % 2)
                sidx += 1
            emit_qk(0, sbs[0])
            for kt in range(nkt):
                n = kt // 2
                sb = sbs[kt]
                ps = c.ps_s[sb]
                if kt + 1 < nkt:
                    emit_qk(kt + 1, sbs[kt + 1])
                pb = pidx % 3
                pidx += 1
                PT = c.PT[pb]
                if n >= own - 1:
                    which = (n - (own - 1)) * 2 + kt % 2
                    sS = c.sS[sb]
                    P.op("dve", lambda e, ps=ps, sS=sS, which=which, b=b: e.tensor_tensor(out=sS[:, :], in0=ps[:, 0:256], in1=b[:, which, :], op=ALU.add),
                         reads=[("ps_s", sb), ("b", hb)], writes=[("sS", sb)])
                    P.op("act", lambda e, sS=sS, PT=PT: e.activation(out=R(PT[:, :]), in_=sS[:, :], func=AF.Exp), reads=[("sS", sb)], writes=[("PT", pb)])
                else:
                    P.op("act", lambda e, ps=ps, PT=PT: e.activation(out=R(PT[:, :]), in_=ps[:, 0:256], func=AF.Exp), reads=[("ps_s", sb)], writes=[("PT", pb)])
                P.op("pe", lambda e, kt=kt, PT=PT, v=v, nkt=nkt: e.matmul(c.ps_o[:, 0:256], R(v[:, kt, :]), R(PT[:, :]), start=(kt == 0), stop=(kt == nkt - 1)),
                     reads=vk + [("PT", pb)], writes=["ps_o"])
                P.op("pe", lambda e, kt=kt, PT=PT, nkt=nkt: e.matmul(c.ps_l[:, 0:256], R(c.ones[:, :]), R(PT[:, :]), start=(kt == 0), stop=(kt == nkt - 1)),
                     reads=["ones", ("PT", pb)], writes=["ps_l"])
            P.op("dve", lambda e: e.reciprocal(out=c.rl[:, :], in_=c.ps_l[:, 0:256]), reads=["ps_l"], writes=["rl"])
            P.op("dve", lambda e, h=h, qs=qs: e.tensor_tensor(out=R(c.attnT[:, h, qs]), in0=c.ps_o[:, 0:256], in1=c.rl[:, :], op=ALU.mult),
                 reads=["ps_o", "rl"], writes=[("attnT", h)])
    akeys = [("attnT", h) for h in range(NH)]
    for p in range(2):
        cols = slice(p * TP, (p + 1) * TP)
        for ch in range(KC):
            wb = ch % 2
            wt = c.k[wb]
            P.dma("sp", R(wt[:, :, :, :]), R(wo_d[ch]), writes=[("kp", wb), ("kn", wb)], semkey=("dma", "wo", wb))
            ps = c.ps_s[wb]
            for hh in range(NH):
                P.op("pe", lambda e, hh=hh, ps=ps, wt=wt, cols=cols: e.matmul(ps[:, :], R(wt[:, hh // 2, hh % 2, :]), R(c.attnT[:, hh, cols]), start=(hh == 0), stop=(hh == NH - 1)),
                     reads=[("kp", wb), ("kn", wb)] + akeys, writes=[("ps_s", wb)])
            P.op("dve", lambda e, ps=ps, ch=ch, cols=cols: e.scalar_tensor_tensor(out=c.xT[:, ch, cols], in0=ps[:, :], scalar=c.modT[:, 32 + ch:33 + ch],
                                                                                  in1=c.xT[:, ch, cols], op0=ALU.mult, op1=ALU.add),
                 reads=[("ps_s", wb), "modT", ("xT", p, ch)], writes=[("xT", p, ch)])


def ada_body(P, c, io, groups):
    nch = 48
    P.phase()
    w = [P.alloc([128, KC, 128], r=True) for _ in range(4)]
    cond = P.alloc([128, KC, 2], r=True)
    cT = P.alloc([128, KC, 2])
    bT = P.alloc([128, 4, nch])
    modS = P.alloc([128, 4, nch])
    ps = [c.bank[1], c.bank[2], c.bank[3], c.bank[4]]
    P.dma("sp", cT[:, :, :], io["cT"], writes=["cT"])
    P.dma("sp", bT[:, :, :], io["badaT"], writes=["bT"])
    P.op("act", lambda e: e.activation(out=R(cond[:, :, :]), in_=cT[:, :, :], func=AF.Silu), reads=["cT"], writes=["cond"])
    t = 0
    for l in range(4):
        for ch in range(nch):
            wb = t % 4
            t += 1
            P.dma("sp", R(w[wb][:, :, :]), R(io["wada"][l, ch]), writes=[("w", wb)])
            for kc in range(KC):
                P.op("pe", lambda e, kc=kc, wb=wb: e.matmul(ps[wb][:, 0:2], R(w[wb][:, kc, :]), R(cond[:, kc, :]), start=(kc == 0), stop=(kc == KC - 1)),
                     reads=[("w", wb), "cond"], writes=[("ps", wb)])
            P.op("dve", lambda e, wb=wb, l=l, ch=ch: e.tensor_scalar(out=modS[:, l, ch:ch + 1], in0=ps[wb][:, 0:1], scalar1=bT[:, l, ch:ch + 1], scalar2=None, op0=ALU.add),
                 reads=[("ps", wb), "bT"], writes=["modS"])
    P.dma("sp", io["adai"].rearrange("p (l c) -> p l c", c=nch), modS[:, :, :], reads=["modS"], writes=["adai"])
    P.cc("AllGather", groups, io["adai"], io["adao"], ["adai"], ["adao"], "cc_ada")
    for r in range(2):
        P.dma("sp", c.modT_all[:, :, r * nch:(r + 1) * nch], io["adao"][r * 128:(r + 1) * 128, :].rearrange("p (l c) -> p l c", c=nch),
              reads=["adao"], writes=["modT"], semkey=("dma", "modT", r))


def build_fused(ncore, stop_after=99):
    nc = new_nc()
    nch = 48
    groups = [[2 * g, 2 * g + 1] for g in range(ncore // 2)]
    dram = lambda n, s, k="ExternalInput": nc.dram_tensor(n, list(s), F32, kind=k).ap()
    io = {}
    io["xT"] = dram("xT", [128, KC, TC])
    io["cT"] = dram("cT", [128, KC, 2])
    io["scal"] = dram("scal", [128, 8])
    io["ic"] = dram("ic", [128, 4, 4, 16])
    io["cmask"] = dram("cmask", [128, 3, 8, 8])
    io["wada"] = dram("wada", [4, nch, 128, KC, 128])
    io["badaT"] = dram("badaT", [128, 4, nch])
    io["vecs"] = dram("vecs", [4, 128, 64])
    io["convw"] = dram("convw", [4, 128, 3, 2 * FC])
    io["convb"] = dram("convb", [4, 128, 2 * FC])
    io["wup"] = dram("wup", [4, FC, 128, KC, 256])
    io["wdn"] = dram("wdn", [4, NP, 2, 128, NF, 1024])
    io["poolw"] = dram("poolw", [2, 4, 128, 8, 256])
    io["wqkv"] = dram("wqkv", [2, 24, 128, KC, 256])
    io["wo"] = dram("wo", [2, KC, 128, 8, 2, 128])
    io["biasT"] = dram("biasT", [NH, 128, 4, 256])
    io["tab31"] = dram("tab31", [128, NH])
    io["oh"] = dram("oh", [128, 8, 128])
    io["zeros"] = dram("zeros", [128, TC])
    io["ident"] = dram("ident", [128, 128])
    io["yo"] = dram("yo", [128, KC, TC], "ExternalOutput")
    io["xhi"] = dram("xhi", [128, KC * HW], "Internal")
    io["xho"] = dram("xho", [256, KC * HW], "Internal")
    io["adai"] = dram("adai", [128, 4 * nch], "Internal")
    io["adao"] = dram("adao", [256, 4 * nch], "Internal")
    io["qo"] = dram("qo", [NH * 128, TC], "Internal")
    io["ko"] = [dram("ko%d" % g, [512, TC], "Internal") for g in range(4)]
    io["vo"] = [dram("vo%d" % g, [TC, 512], "Internal") for g in range(4)]
    io["kex"] = [dram("kex%d" % g, [1024, TC], "Internal") for g in range(4)]
    io["vex"] = [dram("vex%d" % g, [2 * TC, 512], "Internal") for g in range(4)]

    P = Prog(nc)
    c = Ctx()
    c.xT = P.sbuf("xT", [128, KC, TC])
    c.xh = P.sbuf("xh", [128, KC, HW])
    c.modT_all = P.sbuf("modT", [128, 4, 96])
    c.scal = P.sbuf("scal", [128, 8])
    P.setup_arenas(32000, 3700)
    c.bank = [P.psum("bank%d" % k, [128, 512]) for k in range(7)]
    P.dma("sp", c.xT[:, :, 0:TP], io["xT"][:, :, 0:TP], writes=[("xT", 0, k) for k in range(KC)], semkey="ldx0")
    P.dma("sp", c.xT[:, :, TP:TC], io["xT"][:, :, TP:TC], writes=[("xT", 1, k) for k in range(KC)], semkey="ldx1")
    P.dma("sp", c.scal[:], io["scal"], writes=["scal"])
    ada_body(P, c, io, groups)
    step = 0
    for i in range(4):
        if step >= stop_after:
            break
        if i % 2 == 0:
            layer_body(P, c, io, i, "pool", False, groups)
            step += 1
        else:
            outk = qkv_body(P, c, io, i, groups)
            step += 1
            if step >= stop_after:
                break
            attn_body(P, c, io, i, outk, groups)
            step += 1
            if step >= stop_after:
                break
            layer_body(P, c, io, i, "none", i == 3, groups)
            step += 1
    if stop_after < 99:
        P.phase()
        for p in range(2):
            P.dma("sp", io["yo"][:, :, p * TP:(p + 1) * TP], c.xT[:, :, p * TP:(p + 1) * TP], reads=[("xT", p, k) for k in range(KC)], writes=[("yout", p)])
    P.emit()
    stats = P.stats
    P.close()
    return nc, stats
def to_fm(a):
    T, Dd = a.shape
    return np.ascontiguousarray(a.T.reshape(Dd // 128, 128, T).transpose(1, 0, 2))


def from_fm(a):
    p, k, T = a.shape
    return np.ascontiguousarray(a.transpose(1, 0, 2).reshape(k * p, T).T)


def vec_fm(v):
    return np.ascontiguousarray(v.reshape(-1, 128).T)


def prep_wup(w):
    w = w.reshape(KC, 128, 2, FC, 128)
    return np.ascontiguousarray(w.transpose(3, 1, 0, 2, 4).reshape(FC, 128, KC, 256))


def prep_wdn(w):
    w = w.reshape(NP, NF, 128, 2, 1024)
    return np.ascontiguousarray(w.transpose(0, 3, 2, 1, 4))


def prep_poolw(w):
    w = w.reshape(4, 4, 128, 512).transpose(0, 2, 1, 3)
    return np.ascontiguousarray(w.reshape(4, 128, 8, 256))


def prep_convw(cw):
    return np.ascontiguousarray(cw.reshape(3, 2 * FC, 128).transpose(2, 0, 1))


def prep_ic(first_half):
    ic = np.zeros((128, 4, 4, 16), np.float32)
    for g, w in enumerate(WINS):
        for t in range(16):
            cnt = min(t + 1, w) if first_half else w
            ic[:, g, :, t] = np.float32(1.0) / np.float32(cnt)
    return ic


def prep_scal(first_half):
    s = np.zeros((128, 8), np.float32)
    s[:, 0] = 0.0 if first_half else 1.0
    return s


def _rel_bucket_np(dist):
    n = np.maximum(dist, 0)
    nf = np.maximum(n, 1).astype(np.float32)
    large = 16 + (np.log(nf / np.float32(16)) / np.float32(math.log(128 / 16)) * np.float32(16)).astype(np.int32)
    large = np.minimum(large, 31)
    return np.where(n < 16, n, large)


def prep_bias(rel_table):
    t = np.arange(128)[:, None, None]
    which = np.arange(4)[None, :, None]
    s = np.arange(256)[None, None, :]
    blk = which // 2
    kl = (which % 2) * 128 + t
    dist = np.where(blk == 1, s - kl, s + 256 - kl)
    idx = _rel_bucket_np(dist)
    out = rel_table[idx]
    out = np.where((dist < 0)[..., None], np.float32(NEGB), out)
    return np.ascontiguousarray(out.transpose(3, 0, 1, 2)).astype(np.float32)


def prep_cmask(first_half):
    cm = np.zeros((128, 3, 8, 8), np.float32)
    for qt in range(8):
        own = 4 + qt // 2
        for n in range(8):
            valid = (n < own) and (n >= 4 or not first_half)
            cm[:, 0, qt, n] = 0.0 if valid else -1e30
            cm[:, 1, qt, n] = 0.0 if n == own else -NEGB
            cm[:, 2, qt, n] = 1.0 if n <= own - 2 else 0.0
    return cm


def prep_wqkv(w):
    return np.ascontiguousarray(w.reshape(KC, 128, 24, 256).transpose(2, 1, 0, 3))


def prep_wo(w):
    return np.ascontiguousarray(w.reshape(NH, 128, KC, 128).transpose(2, 1, 0, 3).reshape(KC, 128, 8, 2, 128))


_FUSED = {}


def _prep_inputs(ncore, x, c, norm_g, w_ada, b_ada, pool_w, pool_scale, w_qkv, w_o, rel_table,
                 w_up, conv_w, conv_b, w_down, final_g):
    nch = 48
    W = nch * 128
    vecs = np.stack([np.concatenate([vec_fm(norm_g[i, 0]), vec_fm(norm_g[i, 1]), vec_fm(pool_scale[i // 2]), vec_fm(final_g)], axis=1)
                     for i in range(4)])
    convw = np.stack([prep_convw(conv_w[i]) for i in range(4)])
    convb = np.stack([vec_fm(conv_b[i]) for i in range(4)])
    wup = np.stack([prep_wup(w_up[i]) for i in range(4)])
    wdn = np.stack([prep_wdn(w_down[i]) for i in range(4)])
    poolw = np.stack([prep_poolw(pool_w[l]) for l in range(2)])
    wqkv = np.stack([prep_wqkv(w_qkv[l]) for l in range(2)])
    wo = np.stack([prep_wo(w_o[l]) for l in range(2)])
    biasT = prep_bias(rel_table)
    tab31 = np.ascontiguousarray(np.broadcast_to(rel_table[31][None, :], (128, NH)))
    oh = np.zeros((128, 8, 128), np.float32)
    for n in range(8):
        oh[n, n, :] = 1.0
    ident = np.eye(128, dtype=np.float32)
    zeros = np.zeros((128, TC), np.float32)
    maps = []
    for j in range(ncore):
        b, half = j // 2, j % 2
        wa = w_ada[:, :, half * W:(half + 1) * W].reshape(4, KC, 128, nch, 128).transpose(0, 3, 2, 1, 4)
        ba = b_ada[:, half * W:(half + 1) * W].reshape(4, nch, 128).transpose(2, 0, 1)
        cb = c[b].reshape(KC, 128).T
        cT = np.ascontiguousarray(np.stack([cb, cb], axis=2))
        maps.append(dict(xT=to_fm(x[b, half * TC:(half + 1) * TC]), cT=cT, scal=prep_scal(half == 0), ic=prep_ic(half == 0),
                         cmask=prep_cmask(half == 0), wada=np.ascontiguousarray(wa), badaT=np.ascontiguousarray(ba),
                         vecs=np.ascontiguousarray(vecs), convw=convw, convb=convb, wup=wup, wdn=wdn, poolw=poolw, wqkv=wqkv, wo=wo,
                         biasT=biasT, tab31=tab31, oh=oh, ident=ident, zeros=zeros))
    return maps


def run_fused(ncore, stop_after=99, **inp):
    if ncore not in _FUSED:
        _FUSED[ncore] = build_fused(ncore, stop_after)[0]
    maps = _prep_inputs(ncore, **inp)
    res = run_bass_kernel_spmd(_FUSED[ncore], maps, core_ids=list(range(ncore)))
    return [r["yo"] for r in res.results]


def kernel(x, c, norm_g, w_ada, b_ada, pool_w, pool_scale, w_qkv, w_o, rel_table,
           w_up, conv_w, conv_b, w_down, final_g):
    f32 = lambda a: np.ascontiguousarray(np.asarray(a, dtype=np.float32))
    ys = run_fused(8, x=f32(x), c=f32(c), norm_g=f32(norm_g), w_ada=f32(w_ada), b_ada=f32(b_ada), pool_w=f32(pool_w),
                   pool_scale=f32(pool_scale), w_qkv=f32(w_qkv), w_o=f32(w_o), rel_table=f32(rel_table), w_up=f32(w_up),
                   conv_w=f32(conv_w), conv_b=f32(conv_b), w_down=f32(w_down), final_g=f32(final_g))
    out = np.zeros((4, 2 * TC, D), np.float32)
    for j in range(8):
        out[j // 2, (j % 2) * TC:(j % 2 + 1) * TC] = from_fm(ys[j])
    return out
```

```python
from contextlib import ExitStack
import math
import numpy as np
import concourse.bass as bass
import concourse.mybir as mybir
from concourse.bass_utils import run_bass_kernel_spmd

F32 = mybir.dt.float32
F32R = mybir.dt.float32r
AF = mybir.ActivationFunctionType
ALU = mybir.AluOpType
AX = mybir.AxisListType

ENGS = ("pe", "act", "dve", "pool", "sp")

D = 2048
KC = 16
FF = 5632
FC = 44
NF = 4
NP = FC // NF
TP = 512
TC = 1024
HW = 32
NH = 16
EPS = 1e-6
WINS = (2, 4, 8, 16)
NEGB = -30000.0


class Prog:
    def __init__(self, nc, same_engine_sync=True):
        self.nc = nc
        self.ops = []
        self.same_engine_sync = same_engine_sync
        self.stack = ExitStack()
        self.arR = self.arF = None

    def sbuf(self, name, shape, dtype=F32):
        return self.stack.enter_context(self.nc.sbuf_tensor("sb_" + name, list(shape), dtype))

    def psum(self, name, shape, dtype=F32):
        return self.stack.enter_context(self.nc.psum_tensor("pp_" + name, list(shape), dtype))

    def setup_arenas(self, words_r, words_f):
        self.capR, self.capF = words_r, words_f
        self.arR = self.sbuf("arenaR", [128, words_r])
        self.arF = self.sbuf("arenaF", [128, words_f])
        self.offR = self.offF = 0

    def phase(self):
        self.offR = self.offF = 0
        self.ops.append(dict(barrier=True))

    def alloc(self, shape, r=False):
        words = 1
        for s_ in shape[1:]:
            words *= s_
        if r:
            off, ar, cap = self.offR, self.arR, self.capR
            self.offR += words
        else:
            off, ar, cap = self.offF, self.arF, self.capF
            self.offF += words
        assert off + words <= cap, ("arena overflow", r, off, words, cap)
        v = ar[0:shape[0], off:off + words]
        if len(shape) == 3:
            v = v.rearrange("p (a b) -> p a b", b=shape[2])
        elif len(shape) == 4:
            v = v.rearrange("p (a b c) -> p a b c", b=shape[2], c=shape[3])
        return v

    def op(self, eng, fn, reads=(), writes=()):
        self.ops.append(dict(eng=eng, fn=fn, reads=tuple(reads), writes=tuple(writes),
                             dma=False, semkey=None))

    def dma(self, q, out, in_, reads=(), writes=(), semkey=None, **kw):
        writes = tuple(writes)
        reads = tuple(reads)
        if semkey is None:
            semkey = ("dma", writes[0])
        self.ops.append(dict(eng=q, fn=lambda e: e.dma_start(out=out, in_=in_, **kw),
                             reads=reads, writes=writes, dma=True, semkey=semkey))

    def cc(self, kind, groups, in_ap, out_ap, reads, writes, semkey):
        self.ops.append(dict(eng="pool", fn=lambda e: e.collective_compute(kind, ALU.bypass, replica_groups=groups, ins=[in_ap], outs=[out_ap]),
                             reads=tuple(reads), writes=tuple(writes), dma=True, semkey=semkey, inc=1))

    def emit(self, final_wait_eng="sp"):
        nc = self.nc
        allops = self.ops
        ops = []
        bar_before = set()
        for o in allops:
            if o.get("barrier"):
                bar_before.add(len(ops))
            else:
                ops.append(o)
        n = len(ops)
        last_w = {}
        readers = {}
        deps = [None] * n
        last_on_eng = {}
        last_dma_key = {}
        pending_bar = {}
        for i, o in enumerate(ops):
            if i in bar_before:
                snap = set(last_on_eng.values()) | set(last_dma_key.values())
                for e in ENGS:
                    pending_bar[e] = set(snap)
            d = set()
            for r in o["reads"]:
                if r in last_w:
                    d.add(last_w[r])
            for w in o["writes"]:
                if w in last_w:
                    d.add(last_w[w])
                d.update(readers.get(w, ()))
            if pending_bar.get(o["eng"]):
                d.update(pending_bar[o["eng"]])
                pending_bar[o["eng"]] = None
            d.discard(i)
            deps[i] = d
            for r in o["reads"]:
                readers.setdefault(r, []).append(i)
            for w in o["writes"]:
                last_w[w] = i
                readers[w] = []
            if o["dma"]:
                last_dma_key[o["semkey"]] = i
            else:
                last_on_eng[o["eng"]] = i
        needed = [False] * n
        for i, o in enumerate(ops):
            best = {}
            pruned = set()
            for j in deps[i]:
                oj = ops[j]
                if oj["dma"]:
                    pruned.add(j)
                elif best.get(oj["eng"], -1) < j:
                    best[oj["eng"]] = j
            pruned.update(best.values())
            deps[i] = pruned
        for i, o in enumerate(ops):
            keep = set()
            for j in deps[i]:
                oj = ops[j]
                if oj["dma"]:
                    keep.add(j)
                    continue
                if oj["eng"] == o["eng"] and not o["dma"]:
                    if o["eng"] == "pe" or not self.same_engine_sync:
                        continue
                keep.add(j)
                needed[j] = True
            deps[i] = keep
        eng_sem = {e: self.stack.enter_context(nc.semaphore("s_" + e)) for e in ENGS}
        dma_sem = {}
        dma_cnt = {}
        tok = [None] * n
        eng_cnt = {e: 0 for e in ENGS}
        for i, o in enumerate(ops):
            if o["dma"]:
                k = o["semkey"]
                if k not in dma_sem:
                    dma_sem[k] = self.stack.enter_context(nc.semaphore("d%d" % len(dma_sem)))
                    dma_cnt[k] = 0
                dma_cnt[k] += o.get("inc", 16)
                tok[i] = (dma_sem[k], dma_cnt[k])
            elif needed[i]:
                eng_cnt[o["eng"]] += 1
                tok[i] = (eng_sem[o["eng"]], eng_cnt[o["eng"]])
        self.stats = dict(n_ops=n, n_dma_sems=len(dma_sem), eng_cnt=dict(eng_cnt), max_dma=max(dma_cnt.values()) if dma_cnt else 0)
        per_eng = {e: [] for e in ENGS}
        for i, o in enumerate(ops):
            per_eng[o["eng"]].append(i)
        block = self.stack.enter_context(nc.Block())

        def run(e_name, eng):
            waited = {}
            for i in per_eng[e_name]:
                o = ops[i]
                ws = {}
                for j in deps[i]:
                    s, v = tok[j]
                    key = id(s)
                    if key not in ws or ws[key][1] < v:
                        ws[key] = (s, v)
                for key, (s, v) in ws.items():
                    if waited.get(key, 0) >= v:
                        continue
                    eng.wait_ge(s, v)
                    waited[key] = v
                ins = o["fn"](eng)
                if tok[i] is not None:
                    ins.then_inc(tok[i][0], o.get("inc", 16) if o["dma"] else 1)
            if e_name == final_wait_eng:
                for k, s in dma_sem.items():
                    if waited.get(id(s), 0) < dma_cnt[k]:
                        eng.wait_ge(s, dma_cnt[k])

        @block.sync
        def _(eng):
            run("sp", eng)

        @block.scalar
        def _(eng):
            run("act", eng)

        @block.vector
        def _(eng):
            run("dve", eng)

        @block.gpsimd
        def _(eng):
            run("pool", eng)

        @block.tensor
        def _(eng):
            run("pe", eng)

    def close(self):
        self.stack.close()


def new_nc():
    nc = bass.Bass("TRN2", target_bir_lowering=False)
    nc.dge_precook = False
    return nc


def R(ap):
    return ap.bitcast(F32R)


class Ctx:
    def fork(self):
        n = Ctx()
        n.__dict__.update(self.__dict__)
        return n


def emit_norm(P, c, xsrc, xkeyf, col0, width, hdst0, A, Bv, tag, mask=None):
    W = width
    ssq = c.ps_n
    ring = c.sqring
    for kc in range(KC):
        sq, sk = ring[kc % len(ring)]
        P.op("act", lambda e, kc=kc, sq=sq: e.activation(out=R(sq[:, 0:W]), in_=xsrc[:, kc, col0:col0 + W], func=AF.Square),
             reads=[xkeyf(kc)], writes=[sk])
        P.op("pe", lambda e, kc=kc, sq=sq: e.matmul(ssq[:, 0:W], R(c.ones[:, :]), R(sq[:, 0:W]), start=(kc == 0), stop=(kc == KC - 1)),
             reads=[sk, "ones"], writes=["ps_n"])
    P.op("dve", lambda e: e.tensor_scalar(out=c.rstd[:, 0:W], in0=ssq[:, 0:W], scalar1=1.0 / D, scalar2=EPS, op0=ALU.mult, op1=ALU.add),
         reads=["ps_n"], writes=["rstd"])
    P.op("act", lambda e: e.activation(out=c.rstd[:, 0:W], in_=c.rstd[:, 0:W], func=AF.Sqrt), reads=["rstd"], writes=["rstd"])
    P.op("dve", lambda e: e.reciprocal(out=c.rstd[:, 0:W], in_=c.rstd[:, 0:W]), reads=["rstd"], writes=["rstd"])
    if mask is not None:
        P.op("dve", lambda e: e.tensor_scalar(out=c.rstd[:, 0:W], in0=c.rstd[:, 0:W], scalar1=mask, scalar2=None, op0=ALU.mult),
             reads=["rstd", "scal"], writes=["rstd"])
    for kc in range(KC):
        tb = "tbuf%d" % (kc % 2)
        tt = c.ntmp[kc % 2]
        P.op("pool" if kc % 2 == 0 else "dve", lambda e, kc=kc, tt=tt: e.tensor_tensor(out=tt[:, 0:W], in0=xsrc[:, kc, col0:col0 + W], in1=c.rstd[:, 0:W], op=ALU.mult),
             reads=[xkeyf(kc), "rstd"], writes=[tb])
        if mask is None:
            P.op("act", lambda e, kc=kc, tt=tt: e.activation(out=R(c.hT[:, kc, hdst0:hdst0 + W]), in_=tt[:, 0:W], func=AF.Identity,
                                                             bias=Bv[:, kc:kc + 1], scale=A[:, kc:kc + 1]),
                 reads=[tb, tag], writes=[("hT", kc)])
        else:
            P.op("act", lambda e, kc=kc, tt=tt: e.activation(out=tt[:, 0:W], in_=tt[:, 0:W], func=AF.Identity,
                                                             bias=Bv[:, kc:kc + 1], scale=A[:, kc:kc + 1]),
                 reads=[tb, tag], writes=[tb])
            P.op("dve", lambda e, kc=kc, tt=tt: e.tensor_scalar(out=R(c.hT[:, kc, hdst0:hdst0 + W]), in0=tt[:, 0:W], scalar1=mask, scalar2=None, op0=ALU.mult),
                 reads=[tb, "scal"], writes=[("hT", kc)])


def emit_ffn_pass(P, c, p, ffn_w):
    wup_d, wdn_d = ffn_w
    cols = slice(p * TP, (p + 1) * TP)
    hkeys = [("hT", kc) for kc in range(KC)]

    def up_chunk(fc):
        wb = fc % 2
        wt = c.wup[wb]
        P.dma("sp", R(wt[:]), R(wup_d[fc]), writes=[("wup", wb)])
        for half, (pm, ph) in enumerate(((c.ps_v, c.ps_hv), (c.ps_g, c.ps_hg))):
            pk = "ps_v" if half == 0 else "ps_g"
            phk = "ps_hv" if half == 0 else "ps_hg"
            for kc in range(KC):
                P.op("pe", lambda e, kc=kc, pm=pm, half=half, wt=wt: e.matmul(pm[:, :], R(wt[:, kc, half * 128:(half + 1) * 128]),
                                                                              R(c.hT[:, kc, HW:HW + TP]), start=(kc == 0), stop=(kc == KC - 1)),
                     reads=[("wup", wb)] + hkeys, writes=[pk])
            ub = c.ubuf[half]
            uk = "ubuf%d" % half
            j = fc + half * FC
            if p == 0:
                for kc in range(KC):
                    P.op("pe", lambda e, kc=kc, ph=ph, half=half, wt=wt: e.matmul(ph[:, 0:2], R(wt[:, kc, half * 128:(half + 1) * 128]),
                                                                                  R(c.hT[:, kc, HW - 2:HW]), start=(kc == 0), stop=(kc == KC - 1)),
                         reads=[("wup", wb)] + hkeys, writes=[phk])
            P.op("act", lambda e, ub=ub, pm=pm: e.copy(out=ub[:, 2:2 + TP], in_=pm[:, :]), reads=[pk], writes=[uk])
            if p == 0:
                P.op("act", lambda e, ub=ub, ph=ph: e.copy(out=ub[:, 0:2], in_=ph[:, 0:2]), reads=[phk], writes=[uk + "h"])
                P.op("pool", lambda e, ub=ub, j=j: e.tensor_copy(out=c.usave[:, j, :], in_=ub[:, TP:TP + 2]), reads=[uk], writes=[("usave", j)])
            else:
                P.op("pool", lambda e, ub=ub, j=j: e.tensor_copy(out=ub[:, 0:2], in_=c.usave[:, j, :]), reads=[("usave", j)], writes=[uk + "h"])
            tt = c.tbuf[half]
            tk = "tbuf%d" % half
            P.op("act", lambda e, pm=pm, tt=tt, j=j: e.activation(out=tt[:, :], in_=pm[:, :], func=AF.Identity,
                                                                bias=c.convb[:, j:j + 1], scale=c.convw[:, 2, j:j + 1]),
                 reads=[pk, "convw"], writes=[tk])
            P.op("dve", lambda e, ub=ub, tt=tt, j=j: e.scalar_tensor_tensor(out=tt[:, :], in0=ub[:, 1:1 + TP], scalar=c.convw[:, 1, j:j + 1],
                                                                            in1=tt[:, :], op0=ALU.mult, op1=ALU.add),
                 reads=[uk, uk + "h", tk, "convw"], writes=[tk])
            P.op("dve", lambda e, ub=ub, tt=tt, j=j: e.scalar_tensor_tensor(out=tt[:, :], in0=ub[:, 0:TP], scalar=c.convw[:, 0, j:j + 1],
                                                                            in1=tt[:, :], op0=ALU.mult, op1=ALU.add),
                 reads=[uk, uk + "h", tk, "convw"], writes=[tk])
        P.op("act", lambda e: e.activation(out=R(c.sil[:, :]), in_=c.tbuf[1][:, :], func=AF.Silu), reads=["tbuf1"], writes=["sil"])
        ab = fc % (2 * NF)
        P.op("pool", lambda e, ab=ab: e.tensor_tensor(out=R(c.aT[:, ab, :]), in0=c.sil[:, :], in1=c.tbuf[0][:, :], op=ALU.mult),
             reads=["sil", "tbuf0"], writes=[("aT", ab)])

    def down_piece(pc):
        for dh in range(2):
            wb = dh
            wt = c.wdn[wb]
            P.dma("sp", R(wt[:]), R(wdn_d[pc, dh]), writes=[("wdn", wb)])
            for dl in range(8):
                dch = dh * 8 + dl
                pb = dch % 2
                pd = c.ps_d[pb]
                for f in range(NF):
                    ab = (pc * NF + f) % (2 * NF)
                    P.op("pe", lambda e, f=f, ab=ab, pd=pd, dl=dl, wt=wt: e.matmul(pd[:, :], R(wt[:, f, dl * 128:(dl + 1) * 128]), R(c.aT[:, ab, :]),
                                                                                    start=(f == 0), stop=(f == NF - 1)),
                         reads=[("wdn", wb), ("aT", ab)], writes=[("ps_d", pb)])
                if dch % 2 == 0:
                    P.op("dve", lambda e, pd=pd, dch=dch: e.scalar_tensor_tensor(out=c.xT[:, dch, cols], in0=pd[:, :], scalar=c.modT[:, 80 + dch:81 + dch],
                                                                                  in1=c.xT[:, dch, cols], op0=ALU.mult, op1=ALU.add),
                         reads=[("ps_d", pb), "modT", ("xT", p, dch)], writes=[("xT", p, dch)])
                else:
                    P.op("act", lambda e, pd=pd, dch=dch: e.activation(out=c.dtmp[:, :], in_=pd[:, :], func=AF.Copy, scale=c.modT[:, 80 + dch:81 + dch]),
                         reads=[("ps_d", pb), "modT"], writes=["rstd"])
                    P.op("pool", lambda e, dch=dch: e.tensor_tensor(out=c.xT[:, dch, cols], in0=c.dtmp[:, :], in1=c.xT[:, dch, cols], op=ALU.add),
                         reads=["rstd", ("xT", p, dch)], writes=[("xT", p, dch)])

    for pc in range(NP):
        for f in range(NF):
            up_chunk(pc * NF + f)
        if pc >= 1:
            down_piece(pc - 1)
    down_piece(NP - 1)


def emit_final_norm(P, c, p, ydst):
    emit_norm(P, c, c.xT, (lambda kc, p=p: ("xT", p, kc)), p * TP, TP, HW, c.vecs[:, 48:64], c.zeros16, "vecs")
    P.dma("sp", ydst[:, :, p * TP:(p + 1) * TP], c.hT[:, :, HW:HW + TP], reads=[("hT", kc) for kc in range(KC)], writes=[("yout", p)])


def emit_ones(P, c):
    P.op("pool", lambda e: e.memset(c.ones0[:, :], 1.0), writes=["ones0"])
    P.op("dve", lambda e: e.tensor_copy(out=R(c.ones[:, :]), in_=c.ones0[:, :]), reads=["ones0"], writes=["ones"])


def halo_exchange(P, c, io, groups):
    xk = [("xT", 1, k) for k in range(KC)]
    P.dma("sp", io["xhi"].rearrange("p (k t) -> p k t", t=HW), c.xT[:, :, TC - HW:TC], reads=xk, writes=["xhi"])
    P.cc("AllGather", groups, io["xhi"], io["xho"], ["xhi"], ["xho"], "cc_xh")
    P.dma("sp", c.xh[:, :, :], io["xho"][0:128, :].rearrange("p (k t) -> p k t", t=HW), reads=["xho"], writes=["xh"])


def layer_body(P, c, io, i, mixer, final, groups):
    li = i // 2
    c = c.fork()
    P.phase()
    c.hT = P.alloc([128, KC, HW + TP], r=True)
    c.wup = [P.alloc([128, KC, 256], r=True) for _ in range(2)]
    c.wdn = [P.alloc([128, NF, 1024], r=True) for _ in range(2)]
    c.aT = P.alloc([128, 2 * NF, TP], r=True)
    c.dT = P.alloc([128, 4, HW + TP], r=True)
    c.sil = P.alloc([128, TP], r=True)
    c.sqring = [(c.aT[:, j, :], ("aT", j)) for j in range(2 * NF)]
    c.ones = P.alloc([128, 128], r=True)
    c.ubuf = [P.alloc([128, 2 + TP]) for _ in range(2)]
    c.tbuf = [P.alloc([128, TP]) for _ in range(2)]
    c.ntmp = c.tbuf
    c.rstd = P.alloc([128, TP])
    c.dtmp = c.rstd
    c.fix = P.alloc([128, 4, 16])
    c.usave = P.alloc([128, 2 * FC, 2])
    c.convw = P.alloc([128, 3, 2 * FC])
    c.convb = P.alloc([128, 2 * FC])
    c.ic = P.alloc([128, 4, 4, 16])
    c.ones0 = P.alloc([128, 128])
    c.AB = P.alloc([128, 4, 16])
    c.zeros16 = P.alloc([128, 16])
    c.vecs = P.alloc([128, 64])
    c.pw = [c.wup[k][:, 0:8, :] for k in range(2)]
    c.sA = c.wdn[0][:, :, 0:HW + TP]
    c.sB = c.wdn[1][:, :, 0:HW + TP]
    c.ps_n, c.ps_v, c.ps_g, c.ps_hv, c.ps_hg = c.bank[0], c.bank[1], c.bank[2], c.bank[3], c.bank[4]
    c.ps_d = [c.bank[5], c.bank[6]]
    c.ps_p = c.ps_d
    c.ps_ph = c.ps_hv
    c.modT = c.modT_all[:, i, :]
    hmask = c.scal[:, 0:1]

    P.dma("sp", c.vecs[:, :], io["vecs"][i], writes=["vecs"])
    P.dma("sp", c.convw[:, :, :], io["convw"][i], writes=["convw"])
    P.dma("sp", c.convb[:, :], io["convb"][i], writes=["convw"], semkey=("dma", "convb"))
    emit_ones(P, c)
    P.op("pool", lambda e: e.memset(c.zeros16[:, :], 0.0), writes=["zeros16"])
    P.op("dve", lambda e: e.scalar_tensor_tensor(out=c.AB[:, 0, :], in0=c.modT[:, 16:32], scalar=1.0, in1=c.vecs[:, 0:16], op0=ALU.add, op1=ALU.mult),
         reads=["modT", "vecs"], writes=["AB"])
    P.op("dve", lambda e: e.scalar_tensor_tensor(out=c.AB[:, 1, :], in0=c.modT[:, 64:80], scalar=1.0, in1=c.vecs[:, 16:32], op0=ALU.add, op1=ALU.mult),
         reads=["modT", "vecs"], writes=["AB"])
    P.op("dve", lambda e: e.tensor_tensor(out=c.AB[:, 2, :], in0=c.modT[:, 32:48], in1=c.vecs[:, 32:48], op=ALU.mult),
         reads=["modT", "vecs"], writes=["AB"])
    halo_exchange(P, c, io, groups)

    if mixer == "pool":
        pw_d = io["poolw"][li]
        P.dma("sp", c.ic[:, :, :, :], io["ic"], writes=["ic"])
        for p in range(2):
            if p == 0:
                emit_norm(P, c, c.xh, (lambda kc: "xh"), 0, HW, 0, c.AB[:, 0, :], c.modT[:, 0:16], "AB", mask=hmask)
            else:
                for kc in range(KC):
                    P.op("pool", lambda e, kc=kc: e.tensor_copy(out=R(c.hT[:, kc, 0:HW]), in_=R(c.hT[:, kc, TP:TP + HW])),
                         reads=[("hT", kc)], writes=[("hT", kc)])
            emit_norm(P, c, c.xT, (lambda kc, p=p: ("xT", p, kc)), p * TP, TP, HW, c.AB[:, 0, :], c.modT[:, 0:16], "AB")
            WT = HW + TP
            for g in range(4):
                hg = c.hT[:, 4 * g:4 * g + 4, :]
                hk = [("hT", 4 * g + k) for k in range(4)]
                bufs = [c.sA, c.sB]
                src, srck = hg, hk
                sh = 1
                for st in range(g + 1):
                    dst = bufs[st % 2]
                    dk = ("wdn", st % 2)
                    P.op("pool", lambda e, dst=dst, src=src, sh=sh: e.tensor_tensor(out=R(dst[:, :, sh:WT]), in0=src[:, :, sh:WT], in1=src[:, :, 0:WT - sh], op=ALU.add),
                         reads=srck, writes=[dk])
                    P.op("pool", lambda e, dst=dst, src=src, sh=sh: e.tensor_copy(out=R(dst[:, :, 0:sh]), in_=src[:, :, 0:sh]),
                         reads=srck, writes=[dk])
                    src, srck = dst, [dk]
                    sh *= 2
                P.op("dve", lambda e, src=src, g=g, hg=hg: e.scalar_tensor_tensor(out=R(c.dT[:, :, :]), in0=src[:, :, :], scalar=1.0 / WINS[g],
                                                                                 in1=hg, op0=ALU.mult, op1=ALU.subtract),
                     reads=srck + hk, writes=["dT"])
                if p == 0:
                    P.op("dve", lambda e, src=src, g=g: e.tensor_tensor(out=c.fix[:, :, :], in0=src[:, :, HW:HW + 16], in1=c.ic[:, g, :, :], op=ALU.mult),
                         reads=srck + ["ic"], writes=["fix"])
                    P.op("dve", lambda e, hg=hg: e.tensor_tensor(out=R(c.dT[:, :, HW:HW + 16]), in0=c.fix[:, :, :], in1=hg[:, :, HW:HW + 16], op=ALU.subtract),
                         reads=["fix", "dT"] + hk, writes=["dT"])
                wb = g % 2
                P.dma("sp", R(c.pw[wb]), R(pw_d[g]), writes=[("wup", wb)])
                for m in range(4):
                    ch = 4 * g + m
                    pb = ch % 2
                    pd = c.ps_p[pb]
                    for kc in range(4):
                        P.op("pe", lambda e, kc=kc, m=m, pd=pd, wb=wb: e.matmul(pd[:, :], R(c.pw[wb][:, kc * 2 + m // 2, (m % 2) * 128:(m % 2) * 128 + 128]), R(c.dT[:, kc, HW:WT]),
                                                                                start=(kc == 0), stop=(kc == 3)),
                             reads=[("wup", wb), "dT"], writes=[("ps_d", pb)])
                    P.op("dve", lambda e, pd=pd, ch=ch, p=p: e.scalar_tensor_tensor(out=c.xT[:, ch, p * TP:(p + 1) * TP], in0=pd[:, :], scalar=c.AB[:, 2, ch:ch + 1],
                                                                                    in1=c.xT[:, ch, p * TP:(p + 1) * TP], op0=ALU.mult, op1=ALU.add),
                         reads=[("ps_d", pb), "AB", ("xT", p, ch)], writes=[("xT", p, ch)])
                    if p == 0:
                        for kc in range(4):
                            P.op("pe", lambda e, kc=kc, m=m, wb=wb: e.matmul(c.ps_ph[:, 0:HW], R(c.pw[wb][:, kc * 2 + m // 2, (m % 2) * 128:(m % 2) * 128 + 128]), R(c.dT[:, kc, 0:HW]),
                                                                            start=(kc == 0), stop=(kc == 3)),
                                 reads=[("wup", wb), "dT"], writes=["ps_hv"])
                        P.op("dve", lambda e, ch=ch: e.scalar_tensor_tensor(out=c.xh[:, ch, :], in0=c.ps_ph[:, 0:HW], scalar=c.AB[:, 2, ch:ch + 1],
                                                                            in1=c.xh[:, ch, :], op0=ALU.mult, op1=ALU.add),
                             reads=["ps_hv", "AB", "xh"], writes=["xh"])

    for p in range(2):
        if p == 0:
            emit_norm(P, c, c.xh, (lambda kc: "xh"), 0, HW, 0, c.AB[:, 1, :], c.modT[:, 48:64], "AB", mask=hmask)
        emit_norm(P, c, c.xT, (lambda kc, p=p: ("xT", p, kc)), p * TP, TP, HW, c.AB[:, 1, :], c.modT[:, 48:64], "AB")
        emit_ffn_pass(P, c, p, (io["wup"][i], io["wdn"][i]))
    if final:
        for p in range(2):
            emit_final_norm(P, c, p, io["yo"])


def qkv_body(P, c, io, i, groups):
    li = i // 2
    c = c.fork()
    P.phase()
    hT2 = [P.alloc([128, KC, HW + TP], r=True) for _ in range(2)]
    c.w = [P.alloc([128, KC, 256], r=True) for _ in range(3)]
    sqr = P.alloc([128, 4, TP], r=True)
    c.sqring = [(sqr[:, j, :], ("sqr", j)) for j in range(4)]
    c.ones = P.alloc([128, 128], r=True)
    c.tbuf = [P.alloc([128, TP]) for _ in range(2)]
    c.ntmp = c.tbuf
    c.rstd = P.alloc([128, TP])
    c.st = [P.alloc([128, 512]) for _ in range(2)]
    c.ones0 = P.alloc([128, 128])
    c.AB = P.alloc([128, 4, 16])
    c.vecs = P.alloc([128, 64])
    c.ps_n = c.bank[0]
    c.ps = [c.bank[1], c.bank[2], c.bank[3]]
    c.modT = c.modT_all[:, i, :]
    w_d = io["wqkv"][li]
    P.dma("sp", c.vecs[:, :], io["vecs"][i], writes=["vecs"])
    emit_ones(P, c)
    P.op("dve", lambda e: e.scalar_tensor_tensor(out=c.AB[:, 0, :], in0=c.modT[:, 16:32], scalar=1.0, in1=c.vecs[:, 0:16], op0=ALU.add, op1=ALU.mult),
         reads=["modT", "vecs"], writes=["AB"])
    for p in range(2):
        cp = c.fork()
        cp.hT = hT2[p]
        emit_norm(P, cp, c.xT, (lambda kc, p=p: ("xT", p, kc)), p * TP, TP, HW, c.AB[:, 0, :], c.modT[:, 0:16], "AB")
    hkeys = [("hT", kc) for kc in range(KC)]
    cnt = 0
    outk = []
    scale_q = float(128 ** -0.5)
    for hg in range(24):
        wb = hg % 3
        wt = c.w[wb]
        P.dma("sp", R(wt[:, :, :]), R(w_d[hg]), writes=[("w", wb)])
        for p in range(2):
            hT = hT2[p]
            for m in range(2 if hg < 16 else 4):
                pb = cnt % 3
                sb = cnt % 2
                cnt += 1
                ps = c.ps[pb]
                st = c.st[sb]
                for kc in range(KC):
                    if hg < 16:
                        P.op("pe", lambda e, kc=kc, m=m, ps=ps, wt=wt, hT=hT: e.matmul(ps[:, :], R(wt[:, kc, m * 128:(m + 1) * 128]), R(hT[:, kc, HW:HW + TP]),
                                                                                       start=(kc == 0), stop=(kc == KC - 1)),
                             reads=[("w", wb)] + hkeys, writes=[("ps", pb)])
                    else:
                        P.op("pe", lambda e, kc=kc, m=m, ps=ps, wt=wt, hT=hT: e.matmul(ps[:, 0:256], R(hT[:, kc, HW + m * 128:HW + (m + 1) * 128]), R(wt[:, kc, :]),
                                                                                       start=(kc == 0), stop=(kc == KC - 1)),
                             reads=[("w", wb)] + hkeys, writes=[("ps", pb)])
                if hg < 8:
                    hd = hg * 2 + m
                    P.op("act", lambda e, ps=ps, st=st: e.activation(out=st[:, :], in_=ps[:, :], func=AF.Copy, scale=scale_q), reads=[("ps", pb)], writes=[("st", sb)])
                    dst = io["qo"][hd * 128:(hd + 1) * 128, p * TP:(p + 1) * TP]
                    key = ("qo", hd, p)
                    src = st[:, :]
                elif hg < 16:
                    hd = (hg - 8) * 2 + m
                    P.op("act", lambda e, ps=ps, st=st: e.copy(out=st[:, :], in_=ps[:, :]), reads=[("ps", pb)], writes=[("st", sb)])
                    dst = io["ko"][hd // 4][(hd % 4) * 128:(hd % 4 + 1) * 128, p * TP:(p + 1) * TP]
                    key = ("ko", hd, p)
                    src = st[:, :]
                else:
                    tt = p * 4 + m
                    P.op("act", lambda e, ps=ps, st=st: e.copy(out=st[:, 0:256], in_=ps[:, 0:256]), reads=[("ps", pb)], writes=[("st", sb)])
                    vg = (hg - 16) // 2
                    dst = io["vo"][vg][tt * 128:(tt + 1) * 128, ((hg - 16) % 2) * 256:((hg - 16) % 2 + 1) * 256]
                    key = ("vo", tt, hg)
                    src = st[:, 0:256]
                outk.append(key)
                P.dma("act", dst, src, reads=[("st", sb)], writes=[key], semkey=("o", sb))
        if 8 <= hg < 16 and hg % 2 == 1:
            g = (hg - 8) // 2
            P.cc("AllGather", groups, io["ko"][g], io["kex"][g], [k for k in outk if k[0] == "ko" and k[1] // 4 == g], [("kex", g)], ("cc_k", g))
        if hg >= 16 and hg % 2 == 1:
            g = (hg - 16) // 2
            P.cc("AllGather", groups, io["vo"][g], io["vex"][g], [k for k in outk if k[0] == "vo" and k[2] // 2 - 8 == g], [("vex", g)], ("cc_v", g))
    return outk


def attn_body(P, c, io, i, outk, groups):
    li = i // 2
    kkeys = [k for k in outk if k[0] == "ko"]
    vkeys = [k for k in outk if k[0] == "vo"]
    qkeys = [k for k in outk if k[0] == "qo"]
    c = c.fork()
    P.phase()
    c.attnT = P.alloc([128, NH, TC], r=True)
    c.q = [P.alloc([128, TC], r=True) for _ in range(2)]
    c.k = [P.alloc([128, 8, 2, 128], r=True) for _ in range(2)]
    c.v = [P.alloc([128, 16, 128], r=True) for _ in range(2)]
    c.PT = [P.alloc([128, 256], r=True) for _ in range(3)]
    c.ET2 = [P.alloc([128, TC], r=True) for _ in range(2)]
    c.oh = P.alloc([128, 8, 128], r=True)
    c.kmR = P.alloc([128, 8], r=True)
    c.ones = P.alloc([128, 128], r=True)
    c.b = [P.alloc([128, 4, 256]) for _ in range(2)]
    c.km = P.alloc([128, 8])
    c.gm = P.alloc([128, 8, 8])
    c.g2 = P.alloc([128, 8, 8])
    c.eq = P.alloc([128, 8, 8])
    c.E = P.alloc([128, 8, 8])
    c.mx = P.alloc([128, 8])
    c.sS = [P.alloc([128, 256]) for _ in range(2)]
    c.rl = P.alloc([128, 256])
    c.cm = P.alloc([128, 3, 8, 8])
    c.t31 = P.alloc([128, NH])
    c.ident = P.alloc([128, 128])
    c.ones0 = P.alloc([128, 128])
    c.ps_gate = c.bank[0]
    c.ps_T = [c.bank[1], c.bank[2]]
    c.ps_s = [c.bank[3], c.bank[4], c.bank[7]]
    c.ps_o = c.bank[5]
    c.ps_l = c.bank[6]
    c.modT = c.modT_all[:, i, :]
    q_d, b_d, wo_d = io["qo"], io["biasT"], io["wo"][li]
    P.dma("sp", c.cm[:, :, :, :], io["cmask"], writes=["cm"])
    P.dma("sp", c.t31[:, :], io["tab31"], writes=["t31"])
    P.dma("sp", R(c.oh[:, :, :]), R(io["oh"]), writes=["oh"])
    for t in range(2):
        P.dma("sp", R(c.ET2[t][:, :]), R(io["zeros"]), writes=[("ET", t)])
    P.dma("sp", c.ident[:, :], io["ident"], writes=["ident"])
    emit_ones(P, c)

    def bc(ap2):
        return ap2.unsqueeze(2).to_broadcast([128, 8, 8])

    cnts = dict(s=0, p=0)

    def bufs(h):
        hb = h % 2
        return hb, c.q[hb], c.k[hb], c.v[hb], c.b[hb], [("kp", hb), ("kn", hb)], [("vp", hb), ("vn", hb)]

    def loads(h):
        hb, q, k, v, b, kk, vk = bufs(h)
        hr = slice(h * 128, (h + 1) * 128)
        hg = h // 4
        hq = slice((h % 4) * 128, (h % 4 + 1) * 128)
        P.dma("sp", R(q[:, :]), R(q_d[hr, :]), reads=qkeys, writes=[("q", hb)])
        P.dma("sp", R(k[:, 0:4, :, :]), R(io["kex"][hg][hq, :].rearrange("p (a b c) -> p a b c", b=2, c=128)), reads=[("kex", hg)], writes=[("kp", hb)])
        P.dma("sp", R(k[:, 4:8, :, :]), R(io["ko"][hg][hq, :].rearrange("p (a b c) -> p a b c", b=2, c=128)), reads=kkeys, writes=[("kn", hb)])
        P.dma("sp", R(v[:, 0:8, :]), R(io["vex"][hg][0:TC, hq].rearrange("(a t) d -> t a d", t=128)), reads=[("vex", hg)], writes=[("vp", hb)])
        P.dma("sp", R(v[:, 8:16, :]), R(io["vo"][hg][:, hq].rearrange("(a t) d -> t a d", t=128)), reads=vkeys, writes=[("vn", hb)])
        P.dma("sp", b[:, :, :], b_d[h], writes=[("b", hb)])

    def gate_a(h):
        hb, q, k, v, b, kk, vk = bufs(h)
        P.op("dve", lambda e, k=k: e.tensor_reduce(out=c.km[:, :], in_=k[:, :, :, :], axis=AX.XY, op=ALU.add), reads=kk, writes=["km"])
        P.op("dve", lambda e: e.tensor_scalar(out=R(c.kmR[:, :]), in0=c.km[:, :], scalar1=1.0 / 256.0, scalar2=None, op0=ALU.mult), reads=["km"], writes=["kmR"])
        for qt in range(8):
            P.op("pe", lambda e, qt=qt, q=q: e.matmul(c.ps_gate[:, qt * 8:(qt + 1) * 8], R(q[:, qt * 128:(qt + 1) * 128]), R(c.kmR[:, :]), start=True, stop=True),
                 reads=[("q", hb), "kmR"], writes=["ps_gate"])
        gate3 = c.ps_gate[:, 0:64].rearrange("p (a b) -> p a b", b=8)
        P.op("dve", lambda e, gate3=gate3: e.tensor_tensor(out=c.gm[:, :, :], in0=gate3, in1=c.cm[:, 0, :, :], op=ALU.add), reads=["ps_gate", "cm"], writes=["gm"])
        src = c.gm
        srck = "gm"
        for it in range(2):
            P.op("dve", lambda e, src=src: e.tensor_reduce(out=c.mx[:, :], in_=src[:, :, :], axis=AX.X, op=ALU.max), reads=[srck], writes=["mx"])
            P.op("dve", lambda e, src=src: e.tensor_tensor(out=c.eq[:, :, :], in0=src[:, :, :], in1=bc(c.mx[:, :]), op=ALU.is_ge), reads=[srck, "mx"], writes=["eq"])
            P.op("dve", lambda e, src=src: e.scalar_tensor_tensor(out=c.g2[:, :, :], in0=c.eq[:, :, :], scalar=-1e30, in1=src[:, :, :], op0=ALU.mult, op1=ALU.add),
                 reads=["eq", srck], writes=["g2"])
            src = c.g2
            srck = "g2"
        P.op("dve", lambda e: e.tensor_reduce(out=c.mx[:, :], in_=c.g2[:, :, :], axis=AX.X, op=ALU.max), reads=["g2"], writes=["mx"])
        P.op("dve", lambda e: e.tensor_tensor(out=c.eq[:, :, :], in0=c.gm[:, :, :], in1=bc(c.mx[:, :]), op=ALU.is_ge), reads=["gm", "mx"], writes=["eq"])
        P.op("dve", lambda e: e.tensor_single_scalar(out=c.g2[:, :, :], in_=c.gm[:, :, :], scalar=-1e29, op=ALU.is_gt), reads=["gm"], writes=["g2"])
        P.op("dve", lambda e: e.tensor_tensor(out=c.eq[:, :, :], in0=c.eq[:, :, :], in1=c.g2[:, :, :], op=ALU.mult), reads=["eq", "g2"], writes=["eq"])
        P.op("dve", lambda e: e.scalar_tensor_tensor(out=c.E[:, :, :], in0=c.eq[:, :, :], scalar=-1.0, in1=c.cm[:, 1, :, :], op0=ALU.add, op1=ALU.mult),
             reads=["eq", "cm"], writes=["E"])
        P.op("dve", lambda e, h=h: e.scalar_tensor_tensor(out=c.E[:, :, :], in0=c.cm[:, 2, :, :], scalar=c.t31[:, h:h + 1], in1=c.E[:, :, :], op0=ALU.mult, op1=ALU.add),
             reads=["E", "cm", "t31"], writes=["E"])

    def gate_b(h):
        ET = c.ET2[h % 2]
        for qt in range(8):
            pt = c.ps_T[qt // 4]
            P.op("pe", lambda e, qt=qt, pt=pt: e.transpose(pt[0:8, (qt % 4) * 128:(qt % 4 + 1) * 128], c.E[:, qt, :], c.ident[:, :]),
                 reads=["E", "ident"], writes=[("ps_T", qt // 4)])
        for t in range(2):
            P.op("act", lambda e, t=t, ET=ET: e.copy(out=R(ET[0:8, t * 512:(t + 1) * 512]), in_=c.ps_T[t][0:8, :]), reads=[("ps_T", t)], writes=[("ET", h % 2)])

    def tiles(h, jb):
        hb, q, k, v, b, kk, vk = bufs(h)
        ET = c.ET2[h % 2]
        own = 4 + jb
        qs = slice(jb * 256, (jb + 1) * 256)
        nkt = 2 * (own + 1)

        def emit_qk(kt, sb):
            ps = c.ps_s[sb]
            P.op("pe", lambda e, kt=kt, ps=ps: e.matmul(ps[:, 0:256], R(k[:, kt // 2, kt % 2, :]), R(q[:, qs]), start=True, stop=False),
                 reads=kk + [("q", hb)], writes=[("ps_s", sb)])
            P.op("pe", lambda e, n=kt // 2, ps=ps: e.matmul(ps[:, 0:256], R(c.oh[:, n, :]), R(ET[:, qs]), start=False, stop=True),
                 reads=["oh", ("ET", h % 2)], writes=[("ps_s", sb)])

        sbs = []
        for kt in range(nkt):
            sbs.append(cnts["s"] % 3)
            cnts["s"] += 1
        emit_qk(0, sbs[0])
        emit_qk(1, sbs[1])
        for kt in range(nkt):
            n = kt // 2
            sb = sbs[kt]
            ps = c.ps_s[sb]
            if kt + 2 < nkt:
                emit_qk(kt + 2, sbs[kt + 2])
            pb = cnts["p"] % 3
            cnts["p"] += 1
            PT = c.PT[pb]
            if n >= own - 1:
                which = (n - (own - 1)) * 2 + kt % 2
                sS = c.sS[sb % 2]
                sk = ("sS", sb % 2)
                P.op("dve", lambda e, ps=ps, sS=sS, which=which: e.tensor_tensor(out=sS[:, :], in0=ps[:, 0:256], in1=b[:, which, :], op=ALU.add),
                     reads=[("ps_s", sb), ("b", hb)], writes=[sk])
                P.op("act", lambda e, sS=sS, PT=PT: e.activation(out=R(PT[:, :]), in_=sS[:, :], func=AF.Exp), reads=[sk], writes=[("PT", pb)])
            else:
                P.op("act", lambda e, ps=ps, PT=PT: e.activation(out=R(PT[:, :]), in_=ps[:, 0:256], func=AF.Exp), reads=[("ps_s", sb)], writes=[("PT", pb)])
            P.op("pe", lambda e, kt=kt, PT=PT: e.matmul(c.ps_o[:, 0:256], R(v[:, kt, :]), R(PT[:, :]), start=(kt == 0), stop=(kt == nkt - 1)),
                 reads=vk + [("PT", pb)], writes=["ps_o"])
            P.op("pe", lambda e, kt=kt, PT=PT: e.matmul(c.ps_l[:, 0:256], R(c.ones[:, :]), R(PT[:, :]), start=(kt == 0), stop=(kt == nkt - 1)),
                 reads=["ones", ("PT", pb)], writes=["ps_l"])
        P.op("dve", lambda e: e.reciprocal(out=c.rl[:, :], in_=c.ps_l[:, 0:256]), reads=["ps_l"], writes=["rl"])
        P.op("dve", lambda e: e.tensor_tensor(out=R(c.attnT[:, h, qs]), in0=c.ps_o[:, 0:256], in1=c.rl[:, :], op=ALU.mult),
             reads=["ps_o", "rl"], writes=[("attnT", h)])

    loads(0)
    gate_a(0)
    gate_b(0)
    for h in range(NH):
        if h + 1 < NH:
            loads(h + 1)
        for jb in range(4):
            tiles(h, jb)
            if jb == 1 and h + 1 < NH:
                gate_a(h + 1)
        if h + 1 < NH:
            gate_b(h + 1)
    akeys = [("attnT", h) for h in range(NH)]
    for p in range(2):
        cols = slice(p * TP, (p + 1) * TP)
        for ch in range(KC):
            wb = ch % 2
            wt = c.k[wb]
            P.dma("sp", R(wt[:, :, :, :]), R(wo_d[ch]), writes=[("kp", wb), ("kn", wb)], semkey=("dma", "wo", wb))
            ps = c.ps_s[wb]
            for hh in range(NH):
                P.op("pe", lambda e, hh=hh, ps=ps, wt=wt, cols=cols: e.matmul(ps[:, :], R(wt[:, hh // 2, hh % 2, :]), R(c.attnT[:, hh, cols]), start=(hh == 0), stop=(hh == NH - 1)),
                     reads=[("kp", wb), ("kn", wb)] + akeys, writes=[("ps_s", wb)])
            P.op("dve", lambda e, ps=ps, ch=ch, cols=cols: e.scalar_tensor_tensor(out=c.xT[:, ch, cols], in0=ps[:, :], scalar=c.modT[:, 32 + ch:33 + ch],
                                                                                  in1=c.xT[:, ch, cols], op0=ALU.mult, op1=ALU.add),
                 reads=[("ps_s", wb), "modT", ("xT", p, ch)], writes=[("xT", p, ch)])


def ada_body(P, c, io, groups):
    nch = 48
    P.phase()
    w = [P.alloc([128, KC, 128], r=True) for _ in range(4)]
    cond = P.alloc([128, KC, 2], r=True)
    cT = P.alloc([128, KC, 2])
    bT = P.alloc([128, 4, nch])
    modS = P.alloc([128, 4, nch])
    ps = [c.bank[1], c.bank[2], c.bank[3], c.bank[4]]
    P.dma("sp", cT[:, :, :], io["cT"], writes=["cT"])
    P.dma("sp", bT[:, :, :], io["badaT"], writes=["bT"])
    P.op("act", lambda e: e.activation(out=R(cond[:, :, :]), in_=cT[:, :, :], func=AF.Silu), reads=["cT"], writes=["cond"])
    t = 0
    for l in range(4):
        for ch in range(nch):
            wb = t % 4
            t += 1
            P.dma("sp", R(w[wb][:, :, :]), R(io["wada"][l, ch]), writes=[("w", wb)])
            for kc in range(KC):
                P.op("pe", lambda e, kc=kc, wb=wb: e.matmul(ps[wb][:, 0:2], R(w[wb][:, kc, :]), R(cond[:, kc, :]), start=(kc == 0), stop=(kc == KC - 1)),
                     reads=[("w", wb), "cond"], writes=[("ps", wb)])
            P.op("dve", lambda e, wb=wb, l=l, ch=ch: e.tensor_scalar(out=modS[:, l, ch:ch + 1], in0=ps[wb][:, 0:1], scalar1=bT[:, l, ch:ch + 1], scalar2=None, op0=ALU.add),
                 reads=[("ps", wb), "bT"], writes=["modS"])
    P.dma("sp", io["adai"].rearrange("p (l c) -> p l c", c=nch), modS[:, :, :], reads=["modS"], writes=["adai"])
    P.cc("AllGather", groups, io["adai"], io["adao"], ["adai"], ["adao"], "cc_ada")
    for r in range(2):
        P.dma("sp", c.modT_all[:, :, r * nch:(r + 1) * nch], io["adao"][r * 128:(r + 1) * 128, :].rearrange("p (l c) -> p l c", c=nch),
              reads=["adao"], writes=["modT"], semkey=("dma", "modT", r))


def build_fused(ncore, stop_after=99):
    nc = new_nc()
    nch = 48
    groups = [[2 * g, 2 * g + 1] for g in range(ncore // 2)]
    dram = lambda n, s, k="ExternalInput": nc.dram_tensor(n, list(s), F32, kind=k).ap()
    io = {}
    io["xT"] = dram("xT", [128, KC, TC])
    io["cT"] = dram("cT", [128, KC, 2])
    io["scal"] = dram("scal", [128, 8])
    io["ic"] = dram("ic", [128, 4, 4, 16])
    io["cmask"] = dram("cmask", [128, 3, 8, 8])
    io["wada"] = dram("wada", [4, nch, 128, KC, 128])
    io["badaT"] = dram("badaT", [128, 4, nch])
    io["vecs"] = dram("vecs", [4, 128, 64])
    io["convw"] = dram("convw", [4, 128, 3, 2 * FC])
    io["convb"] = dram("convb", [4, 128, 2 * FC])
    io["wup"] = dram("wup", [4, FC, 128, KC, 256])
    io["wdn"] = dram("wdn", [4, NP, 2, 128, NF, 1024])
    io["poolw"] = dram("poolw", [2, 4, 128, 8, 256])
    io["wqkv"] = dram("wqkv", [2, 24, 128, KC, 256])
    io["wo"] = dram("wo", [2, KC, 128, 8, 2, 128])
    io["biasT"] = dram("biasT", [NH, 128, 4, 256])
    io["tab31"] = dram("tab31", [128, NH])
    io["oh"] = dram("oh", [128, 8, 128])
    io["zeros"] = dram("zeros", [128, TC])
    io["ident"] = dram("ident", [128, 128])
    io["yo"] = dram("yo", [128, KC, TC], "ExternalOutput")
    io["xhi"] = dram("xhi", [128, KC * HW], "Internal")
    io["xho"] = dram("xho", [256, KC * HW], "Internal")
    io["adai"] = dram("adai", [128, 4 * nch], "Internal")
    io["adao"] = dram("adao", [256, 4 * nch], "Internal")
    io["qo"] = dram("qo", [NH * 128, TC], "Internal")
    io["ko"] = [dram("ko%d" % g, [512, TC], "Internal") for g in range(4)]
    io["vo"] = [dram("vo%d" % g, [TC, 512], "Internal") for g in range(4)]
    io["kex"] = [dram("kex%d" % g, [1024, TC], "Internal") for g in range(4)]
    io["vex"] = [dram("vex%d" % g, [2 * TC, 512], "Internal") for g in range(4)]

    P = Prog(nc)
    c = Ctx()
    c.xT = P.sbuf("xT", [128, KC, TC])
    c.xh = P.sbuf("xh", [128, KC, HW])
    c.modT_all = P.sbuf("modT", [128, 4, 96])
    c.scal = P.sbuf("scal", [128, 8])
    P.setup_arenas(32000, 3700)
    c.bank = [P.psum("bank%d" % k, [128, 512]) for k in range(8)]
    P.dma("sp", c.xT[:, :, 0:TP], io["xT"][:, :, 0:TP], writes=[("xT", 0, k) for k in range(KC)], semkey="ldx0")
    P.dma("sp", c.xT[:, :, TP:TC], io["xT"][:, :, TP:TC], writes=[("xT", 1, k) for k in range(KC)], semkey="ldx1")
    P.dma("sp", c.scal[:], io["scal"], writes=["scal"])
    ada_body(P, c, io, groups)
    step = 0
    for i in range(4):
        if step >= stop_after:
            break
        if i % 2 == 0:
            layer_body(P, c, io, i, "pool", False, groups)
            step += 1
        else:
            outk = qkv_body(P, c, io, i, groups)
            step += 1
            if step >= stop_after:
                break
            attn_body(P, c, io, i, outk, groups)
            step += 1
            if step >= stop_after:
                break
            layer_body(P, c, io, i, "none", i == 3, groups)
            step += 1
    if stop_after < 99:
        P.phase()
        for p in range(2):
            P.dma("sp", io["yo"][:, :, p * TP:(p + 1) * TP], c.xT[:, :, p * TP:(p + 1) * TP], reads=[("xT", p, k) for k in range(KC)], writes=[("yout", p)])
    P.emit()
    stats = P.stats
    P.close()
    return nc, stats
def to_fm(a):
    T, Dd = a.shape
    return np.ascontiguousarray(a.T.reshape(Dd // 128, 128, T).transpose(1, 0, 2))


def from_fm(a):
    p, k, T = a.shape
    return np.ascontiguousarray(a.transpose(1, 0, 2).reshape(k * p, T).T)


def vec_fm(v):
    return np.ascontiguousarray(v.reshape(-1, 128).T)


def prep_wup(w):
    w = w.reshape(KC, 128, 2, FC, 128)
    return np.ascontiguousarray(w.transpose(3, 1, 0, 2, 4).reshape(FC, 128, KC, 256))


def prep_wdn(w):
    w = w.reshape(NP, NF, 128, 2, 1024)
    return np.ascontiguousarray(w.transpose(0, 3, 2, 1, 4))


def prep_poolw(w):
    w = w.reshape(4, 4, 128, 512).transpose(0, 2, 1, 3)
    return np.ascontiguousarray(w.reshape(4, 128, 8, 256))


def prep_convw(cw):
    return np.ascontiguousarray(cw.reshape(3, 2 * FC, 128).transpose(2, 0, 1))


def prep_ic(first_half):
    ic = np.zeros((128, 4, 4, 16), np.float32)
    for g, w in enumerate(WINS):
        for t in range(16):
            cnt = min(t + 1, w) if first_half else w
            ic[:, g, :, t] = np.float32(1.0) / np.float32(cnt)
    return ic


def prep_scal(first_half):
    s = np.zeros((128, 8), np.float32)
    s[:, 0] = 0.0 if first_half else 1.0
    return s


def _rel_bucket_np(dist):
    n = np.maximum(dist, 0)
    nf = np.maximum(n, 1).astype(np.float32)
    large = 16 + (np.log(nf / np.float32(16)) / np.float32(math.log(128 / 16)) * np.float32(16)).astype(np.int32)
    large = np.minimum(large, 31)
    return np.where(n < 16, n, large)


def prep_bias(rel_table):
    t = np.arange(128)[:, None, None]
    which = np.arange(4)[None, :, None]
    s = np.arange(256)[None, None, :]
    blk = which // 2
    kl = (which % 2) * 128 + t
    dist = np.where(blk == 1, s - kl, s + 256 - kl)
    idx = _rel_bucket_np(dist)
    out = rel_table[idx]
    out = np.where((dist < 0)[..., None], np.float32(NEGB), out)
    return np.ascontiguousarray(out.transpose(3, 0, 1, 2)).astype(np.float32)


def prep_cmask(first_half):
    cm = np.zeros((128, 3, 8, 8), np.float32)
    for qt in range(8):
        own = 4 + qt // 2
        for n in range(8):
            valid = (n < own) and (n >= 4 or not first_half)
            cm[:, 0, qt, n] = 0.0 if valid else -1e30
            cm[:, 1, qt, n] = 0.0 if n == own else -NEGB
            cm[:, 2, qt, n] = 1.0 if n <= own - 2 else 0.0
    return cm


def prep_wqkv(w):
    return np.ascontiguousarray(w.reshape(KC, 128, 24, 256).transpose(2, 1, 0, 3))


def prep_wo(w):
    return np.ascontiguousarray(w.reshape(NH, 128, KC, 128).transpose(2, 1, 0, 3).reshape(KC, 128, 8, 2, 128))


_FUSED = {}


def _prep_inputs(ncore, x, c, norm_g, w_ada, b_ada, pool_w, pool_scale, w_qkv, w_o, rel_table,
                 w_up, conv_w, conv_b, w_down, final_g):
    nch = 48
    W = nch * 128
    vecs = np.stack([np.concatenate([vec_fm(norm_g[i, 0]), vec_fm(norm_g[i, 1]), vec_fm(pool_scale[i // 2]), vec_fm(final_g)], axis=1)
                     for i in range(4)])
    convw = np.stack([prep_convw(conv_w[i]) for i in range(4)])
    convb = np.stack([vec_fm(conv_b[i]) for i in range(4)])
    wup = np.stack([prep_wup(w_up[i]) for i in range(4)])
    wdn = np.stack([prep_wdn(w_down[i]) for i in range(4)])
    poolw = np.stack([prep_poolw(pool_w[l]) for l in range(2)])
    wqkv = np.stack([prep_wqkv(w_qkv[l]) for l in range(2)])
    wo = np.stack([prep_wo(w_o[l]) for l in range(2)])
    biasT = prep_bias(rel_table)
    tab31 = np.ascontiguousarray(np.broadcast_to(rel_table[31][None, :], (128, NH)))
    oh = np.zeros((128, 8, 128), np.float32)
    for n in range(8):
        oh[n, n, :] = 1.0
    ident = np.eye(128, dtype=np.float32)
    zeros = np.zeros((128, TC), np.float32)
    maps = []
    for j in range(ncore):
        b, half = j // 2, j % 2
        wa = w_ada[:, :, half * W:(half + 1) * W].reshape(4, KC, 128, nch, 128).transpose(0, 3, 2, 1, 4)
        ba = b_ada[:, half * W:(half + 1) * W].reshape(4, nch, 128).transpose(2, 0, 1)
        cb = c[b].reshape(KC, 128).T
        cT = np.ascontiguousarray(np.stack([cb, cb], axis=2))
        maps.append(dict(xT=to_fm(x[b, half * TC:(half + 1) * TC]), cT=cT, scal=prep_scal(half == 0), ic=prep_ic(half == 0),
                         cmask=prep_cmask(half == 0), wada=np.ascontiguousarray(wa), badaT=np.ascontiguousarray(ba),
                         vecs=np.ascontiguousarray(vecs), convw=convw, convb=convb, wup=wup, wdn=wdn, poolw=poolw, wqkv=wqkv, wo=wo,
                         biasT=biasT, tab31=tab31, oh=oh, ident=ident, zeros=zeros))
    return maps


def run_fused(ncore, stop_after=99, **inp):
    if ncore not in _FUSED:
        _FUSED[ncore] = build_fused(ncore, stop_after)[0]
    maps = _prep_inputs(ncore, **inp)
    res = run_bass_kernel_spmd(_FUSED[ncore], maps, core_ids=list(range(ncore)))
    return [r["yo"] for r in res.results]


def kernel(x, c, norm_g, w_ada, b_ada, pool_w, pool_scale, w_qkv, w_o, rel_table,
           w_up, conv_w, conv_b, w_down, final_g):
    f32 = lambda a: np.ascontiguousarray(np.asarray(a, dtype=np.float32))
    ys = run_fused(8, x=f32(x), c=f32(c), norm_g=f32(norm_g), w_ada=f32(w_ada), b_ada=f32(b_ada), pool_w=f32(pool_w),
                   pool_scale=f32(pool_scale), w_qkv=f32(w_qkv), w_o=f32(w_o), rel_table=f32(rel_table), w_up=f32(w_up),
                   conv_w=f32(conv_w), conv_b=f32(conv_b), w_down=f32(w_down), final_g=f32(final_g))
    out = np.zeros((4, 2 * TC, D), np.float32)
    for j in range(8):
        out[j // 2, (j % 2) * TC:(j % 2 + 1) * TC] = from_fm(ys[j])
    return out
```
